# Optimizing a Trainium2 kernel written in Bass

```python
import jax, jax.numpy as jnp
from jax import lax
import numpy as np

D_MODEL = 1024
BATCH = 8
SEQ = 4096
DEPTH = 1

CHUNK = 64
A_HEADS = 8
A_EXPAND = 128
A_FDIM = A_HEADS * A_EXPAND
A_IDIM = D_MODEL
A_HEAD_I = A_IDIM // A_HEADS
B_EXPAND = 2
B_INNER = B_EXPAND * D_MODEL
B_HEADDIM = 64
B_HEADS = B_INNER // B_HEADDIM
B_GROUPS = 4
B_HG = B_HEADS // B_GROUPS
B_STATE = 128
B_CONV = 4
B_CONV_DIM = B_INNER + 2 * B_GROUPS * B_STATE
D_FF = -(-8 * D_MODEL // (3 * 256)) * 256
ALPHA = (2.0 * DEPTH) ** 0.25
BETA = (8.0 * DEPTH) ** -0.25
LN_EPS = 1e-5
RMS_EPS = 1e-6
IN_SPLITS = (A_FDIM, A_FDIM, A_IDIM, A_IDIM, B_INNER, B_CONV_DIM, B_HEADS, D_MODEL, D_MODEL)
IN_DIM = sum(IN_SPLITS)

kernel_name = "hybrid_hgrn2_mamba2_deepnorm_adaln"

F32 = jnp.float32


def layer_norm(x, g=None, b=None):
    x32 = x.astype(F32)
    mu = jnp.mean(x32, axis=-1, keepdims=True)
    xc = x32 - mu
    y = xc * lax.rsqrt(jnp.mean(xc * xc, axis=-1, keepdims=True) + LN_EPS)
    if g is not None:
        y = y * g.astype(F32) + b.astype(F32)
    return y.astype(x.dtype)


def rms_norm(x, w=None):
    x32 = x.astype(F32)
    y = x32 * lax.rsqrt(jnp.mean(x32 * x32, axis=-1, keepdims=True) + RMS_EPS)
    if w is not None:
        y = y * w.astype(F32)
    return y


def to_chunks(t):
    b, s = t.shape[:2]
    return jnp.moveaxis(t.reshape(b, s // CHUNK, CHUNK, *t.shape[2:]), 1, 0)


def from_chunks(t):
    t = jnp.moveaxis(t, 0, 1)
    return t.reshape(t.shape[0], t.shape[1] * t.shape[2], *t.shape[3:])


def causal_dwconv(x, w, b):
    k, ch = w.shape
    y = lax.conv_general_dilated(x, w[:, None, :], window_strides=(1,), padding=[(k - 1, 0)],
                                 dimension_numbers=("NWC", "WIO", "NWC"), feature_group_count=ch)
    return y + b


def hgrn2_mixer(q, f_logit, i, g, lb, w_gnorm):
    bsz, s, _ = q.shape
    f = lb + (1.0 - lb) * jax.nn.sigmoid(f_logit.astype(F32))
    log_f = jnp.log(f)
    k = 1.0 - f
    qf = jax.nn.silu(q.astype(F32)) * (A_EXPAND ** -0.5)
    hk = (bsz, s, A_HEADS, A_EXPAND)
    qh, kh, gh = qf.reshape(hk), k.reshape(hk), log_f.reshape(hk)
    vh = i.astype(F32).reshape(bsz, s, A_HEADS, A_HEAD_I)
    mask = jnp.tril(jnp.ones((CHUNK, CHUNK), bool))[None, :, :, None, None]

    def step(state, inp):
        qc, kc, gc, vc = inp
        bc = jnp.cumsum(gc, axis=1)
        diff = bc[:, :, None] - bc[:, None, :]
        decay = jnp.exp(jnp.where(mask, diff, -jnp.inf))
        scores = jnp.einsum("bthk,bshk,btshk->bhts", qc, kc, decay)
        o = jnp.einsum("bhts,bshv->bthv", scores, vc)
        o = o + jnp.einsum("bthk,bhkv->bthv", qc * jnp.exp(bc), state)
        b_last = bc[:, -1]
        state = state * jnp.exp(b_last)[..., None] + jnp.einsum(
            "bshk,bshv->bhkv", kc * jnp.exp(b_last[:, None] - bc), vc)
        return state, o

    s0 = jnp.zeros((bsz, A_HEADS, A_EXPAND, A_HEAD_I), F32)
    _, o = lax.scan(step, s0, (to_chunks(qh), to_chunks(kh), to_chunks(gh), to_chunks(vh)))
    o = from_chunks(o)
    o = rms_norm(o, w_gnorm) * jax.nn.silu(g.astype(F32)).reshape(o.shape)
    return o.reshape(bsz, s, A_IDIM)


def mamba2_mixer(z, xbc, dt, conv_w, conv_b, dt_bias, a_log, d_skip, w_norm):
    bsz, s, _ = z.shape
    xbc = jax.nn.silu(causal_dwconv(xbc, conv_w, conv_b)).astype(F32)
    xs = xbc[..., :B_INNER].reshape(bsz, s, B_GROUPS, B_HG, B_HEADDIM)
    bm = xbc[..., B_INNER:B_INNER + B_GROUPS * B_STATE].reshape(bsz, s, B_GROUPS, B_STATE)
    cm = xbc[..., B_INNER + B_GROUPS * B_STATE:].reshape(bsz, s, B_GROUPS, B_STATE)
    delta = jax.nn.softplus(dt.astype(F32) + dt_bias.astype(F32)).reshape(bsz, s, B_GROUPS, B_HG)
    a = -jnp.exp(a_log.astype(F32)).reshape(B_GROUPS, B_HG) * delta
    xdt = xs * delta[..., None]
    mask = jnp.tril(jnp.ones((CHUNK, CHUNK), bool))[None, :, :, None, None]

    def step(state, inp):
        xc, ac, bc, cc = inp
        acum = jnp.cumsum(ac, axis=1)
        seg = acum[:, :, None] - acum[:, None, :]
        decay = jnp.exp(jnp.where(mask, seg, -jnp.inf))
        cb = jnp.einsum("btgn,bsgn->btsg", cc, bc)
        y = jnp.einsum("btsg,btsgh,bsghp->btghp", cb, decay, xc)
        y = y + jnp.einsum("btgn,bghpn->btghp", cc, state) * jnp.exp(acum)[..., None]
        a_last = acum[:, -1]
        state = state * jnp.exp(a_last)[..., None, None] + jnp.einsum(
            "bsgn,bsgh,bsghp->bghpn", bc, jnp.exp(a_last[:, None] - acum), xc)
        return state, y

    s0 = jnp.zeros((bsz, B_GROUPS, B_HG, B_HEADDIM, B_STATE), F32)
    _, y = lax.scan(step, s0, (to_chunks(xdt), to_chunks(a), to_chunks(bm), to_chunks(cm)))
    y = from_chunks(y) + xs * d_skip.astype(F32).reshape(B_GROUPS, B_HG)[..., None]
    y = y.reshape(bsz, s, B_INNER) * jax.nn.silu(z.astype(F32))
    y = rms_norm(y.reshape(bsz, s, B_GROUPS, B_INNER // B_GROUPS)).reshape(bsz, s, B_INNER)
    return y * w_norm.astype(F32)


def token_mixer(u, w_in, lb, gnorm, conv_w, conv_b, dt_bias, a_log, d_skip, ssm_norm,
                w_branch_a, w_branch_b, w_o):
    proj = u @ w_in
    offs = np.cumsum(IN_SPLITS)[:-1].tolist()
    q, f, i, g, z, xbc, dt, gate_a, gate_b = jnp.split(proj, offs, axis=-1)
    y_a = hgrn2_mixer(q, f, i, g, lb, gnorm).astype(u.dtype) @ w_branch_a
    y_b = mamba2_mixer(z, xbc, dt, conv_w, conv_b, dt_bias, a_log, d_skip,
                       ssm_norm).astype(u.dtype) @ w_branch_b
    merged = jax.nn.sigmoid(gate_a) * y_a + jax.nn.sigmoid(gate_b) * y_b
    return merged @ w_o


def swiglu(u, w_gate, w_up, w_down):
    return (jax.nn.silu(u @ w_gate) * (u @ w_up)) @ w_down


def setup_inputs(seed: int = 0) -> dict:
    key = jax.random.key(seed)
    ks = jax.random.split(key, 24)
    n = lambda k, shape, s: jax.random.normal(k, shape, F32) * s
    L, D = DEPTH, D_MODEL
    dt0 = jnp.exp(jax.random.uniform(ks[8], (L, B_HEADS), F32, np.log(1e-3), np.log(1e-1)))
    return {
        "x": jax.random.normal(ks[0], (BATCH, SEQ, D), F32),
        "c": jax.random.normal(ks[1], (BATCH, D), F32),
        "w_ada": n(ks[2], (L, D, 6 * D), D ** -0.5),
        "b_ada": n(ks[3], (L, 6 * D), 0.02),
        "w_in": n(ks[4], (L, D, IN_DIM), D ** -0.5),
        "hgrn_lb": n(ks[5], (DEPTH + 1, A_FDIM), 0.1),
        "hgrn_gnorm": 1.0 + n(ks[6], (L, A_HEAD_I), 0.02),
        "ssm_conv_w": n(ks[7], (L, B_CONV, B_CONV_DIM), B_CONV ** -0.5),
        "ssm_conv_b": n(ks[9], (L, B_CONV_DIM), 0.02),
        "ssm_dt_bias": dt0 + jnp.log(-jnp.expm1(-dt0)),
        "ssm_a_log": jnp.log(jax.random.uniform(ks[10], (L, B_HEADS), F32, 1.0, 16.0)),
        "ssm_d": 1.0 + n(ks[11], (L, B_HEADS), 0.02),
        "ssm_norm": 1.0 + n(ks[12], (L, B_INNER), 0.02),
        "w_branch_a": n(ks[13], (L, A_IDIM, D), A_IDIM ** -0.5),
        "w_branch_b": n(ks[14], (L, B_INNER, D), B_INNER ** -0.5),
        "w_o": n(ks[15], (L, D, D), BETA * D ** -0.5),
        "ln1_g": 1.0 + n(ks[16], (L, D), 0.02),
        "ln1_b": n(ks[17], (L, D), 0.02),
        "w_ffn_gate": n(ks[18], (L, D, D_FF), D ** -0.5),
        "w_ffn_up": n(ks[19], (L, D, D_FF), D ** -0.5),
        "w_ffn_down": n(ks[20], (L, D_FF, D), BETA * D_FF ** -0.5),
        "ln2_g": 1.0 + n(ks[21], (L, D), 0.02),
        "ln2_b": n(ks[22], (L, D), 0.02),
    }


def reference(x, c, w_ada, b_ada, w_in, hgrn_lb, hgrn_gnorm, ssm_conv_w, ssm_conv_b,
              ssm_dt_bias, ssm_a_log, ssm_d, ssm_norm, w_branch_a, w_branch_b, w_o,
              ln1_g, ln1_b, w_ffn_gate, w_ffn_up, w_ffn_down, ln2_g, ln2_b):
    cond = jax.nn.silu(c)
    lb_table = jnp.cumsum(jax.nn.softmax(hgrn_lb.astype(F32), axis=0), axis=0)
    for l in range(DEPTH):
        mod = (cond @ w_ada[l] + b_ada[l])[:, None, :]
        sh1, sc1, g1, sh2, sc2, g2 = jnp.split(mod, 6, axis=-1)
        u = layer_norm(x) * (1.0 + sc1) + sh1
        h = token_mixer(u, w_in[l], lb_table[l], hgrn_gnorm[l], ssm_conv_w[l], ssm_conv_b[l],
                        ssm_dt_bias[l], ssm_a_log[l], ssm_d[l], ssm_norm[l],
                        w_branch_a[l], w_branch_b[l], w_o[l])
        x = layer_norm(ALPHA * x + g1 * h, ln1_g[l], ln1_b[l])
        u = layer_norm(x) * (1.0 + sc2) + sh2
        h = swiglu(u, w_ffn_gate[l], w_ffn_up[l], w_ffn_down[l])
        x = layer_norm(ALPHA * x + g2 * h, ln2_g[l], ln2_b[l])
    return x
```

```python
from contextlib import ExitStack
import numpy as np
import concourse.bass as bass
import concourse.mybir as mybir
from concourse.bass_utils import run_bass_kernel_spmd

F32 = mybir.dt.float32
BF16 = mybir.dt.bfloat16
ALU = mybir.AluOpType
AF = mybir.ActivationFunctionType
ENGS = ("pe", "act", "dve", "pool", "sp")

D = 1024
SEQ = 4096
NCORE = 8
ALPHA = 2.0 ** 0.25
LN_EPS = 1e-5
RMS_EPS = 1e-6
D_FF = 2816


class T:
    def __init__(self, name, handle):
        self.name = name
        self.h = handle

    def __getitem__(self, k):
        return self.h[k]


class Op:
    __slots__ = ("eng", "fn", "deps", "is_dma", "dsem", "dcount", "cost", "nbytes", "pos", "count", "signal",
                 "start", "finish", "prio", "users", "tcls")

    def __init__(self, eng, fn, is_dma=False, dsem=None, cost=100.0, nbytes=0):
        self.eng = eng
        self.fn = fn
        self.deps = set()
        self.is_dma = is_dma
        self.dsem = dsem
        self.dcount = None
        self.cost = cost
        self.nbytes = nbytes
        self.pos = None
        self.count = None
        self.signal = False
        self.start = 0.0
        self.finish = 0.0
        self.prio = 0.0
        self.users = []
        self.tcls = None


class Prog:
    def __init__(self, nc):
        self.nc = nc
        self.ops = []
        self.track = {}
        self.stack = ExitStack()
        self.dma_sem_counts = {}

    def sb(self, name, shape, dtype=F32):
        return T("s_" + name, self.stack.enter_context(self.nc.sbuf_tensor("s_" + name, list(shape), dtype)))

    def ps(self, name, shape, dtype=F32):
        return T("p_" + name, self.stack.enter_context(self.nc.psum_tensor("p_" + name, list(shape), dtype)))

    def _conf(self, tn, sub):
        d = self.track.setdefault(tn, {})
        if sub is None:
            return list(d.values())
        out = []
        if sub in d:
            out.append(d[sub])
        if None in d:
            out.append(d[None])
        return out

    @staticmethod
    def _norm(lst):
        out = []
        for it in lst:
            if isinstance(it, tuple):
                t, sub = it
            else:
                t, sub = it, None
            out.append((t.name if isinstance(t, T) else t, sub))
        return out

    def add(self, eng, fn, reads=(), writes=(), is_dma=False, dsem=None, cost=100.0, nbytes=0, tcls=None):
        reads = self._norm(reads)
        writes = self._norm(writes)
        op = Op(eng, fn, is_dma, dsem, cost, nbytes)
        op.tcls = tcls
        idx = len(self.ops)
        for (tn, sub) in reads:
            for rec in self._conf(tn, sub):
                if rec[0] is not None:
                    op.deps.add(rec[0])
                if tn.startswith("p_"):
                    for r_ in rec[1]:
                        if self.ops[r_].eng != eng:
                            op.deps.add(r_)
        for (tn, sub) in writes:
            for rec in self._conf(tn, sub):
                if rec[0] is not None:
                    op.deps.add(rec[0])
                op.deps.update(rec[1])
        for (tn, sub) in reads:
            d = self.track.setdefault(tn, {})
            d.setdefault(sub, [None, []])[1].append(idx)
        for (tn, sub) in writes:
            d = self.track.setdefault(tn, {})
            if sub is None:
                d.clear()
            d[sub] = [idx, []]
        op.deps.discard(idx)
        if is_dma:
            c = self.dma_sem_counts.get(dsem, 0) + 1
            self.dma_sem_counts[dsem] = c
            op.dcount = 16 * c
        self.ops.append(op)
        return idx

    def schedule(self, reorder=True):
        import heapq
        ops = self.ops
        n = len(ops)
        SEM_LAT = 250.0
        if not reorder:
            per = {e: [] for e in ENGS}
            for op in ops:
                per[op.eng].append(op)
            return per
        for i, op in enumerate(ops):
            for d_ in op.deps:
                ops[d_].users.append(i)
        for i in range(n - 1, -1, -1):
            op = ops[i]
            best = 0.0
            for u in op.users:
                if ops[u].prio > best:
                    best = ops[u].prio
            op.prio = best + op.cost + (2000.0 if op.is_dma else 0.0)
        indeg = [len(op.deps) for op in ops]
        ready_t = [0.0] * n
        ready = {e: [] for e in ENGS}
        for i, op in enumerate(ops):
            if indeg[i] == 0:
                heapq.heappush(ready[op.eng], (-op.prio, i))
        free_at = {e: 0.0 for e in ENGS}
        cur_tab = [None]
        TAB_NS = 1283.0
        dma_bw_free = 0.0
        per = {e: [] for e in ENGS}
        done = 0
        WIN = 1
        while done < n:
            best = None
            for e in ENGS:
                h = ready[e]
                if not h:
                    continue
                cands = heapq.nsmallest(WIN_E.get(e, WIN), h)
                for (np_, i) in cands:
                    st = max(free_at[e], ready_t[i])
                    if e == "act" and ops[i].tcls is not None and ops[i].tcls != cur_tab[0]:
                        st += TAB_NS
                    key = (st, np_)
                    if best is None or key < best[0]:
                        best = (key, e, i, np_)
            assert best is not None, "scheduler deadlock"
            (st, _), e, i, np_ = best
            ready[e].remove((np_, i))
            heapq.heapify(ready[e])
            op = ops[i]
            op.start = st
            if op.is_dma:
                free_at[e] = st + 60.0
                t0 = max(st + 1500.0, dma_bw_free)
                op.finish = t0 + op.nbytes / 250.0
                dma_bw_free = op.finish
            else:
                if e == "act" and op.tcls is not None:
                    cur_tab[0] = op.tcls
                op.finish = st + op.cost
                free_at[e] = op.finish
            per[e].append(op)
            done += 1
            for u in op.users:
                lat = 0.0 if (ops[u].eng == e and not op.is_dma and e == "pe") else SEM_LAT
                if op.finish + lat > ready_t[u]:
                    ready_t[u] = op.finish + lat
                indeg[u] -= 1
                if indeg[u] == 0:
                    heapq.heappush(ready[ops[u].eng], (-ops[u].prio, u))
        self.est_ns = max(op.finish for op in ops)
        return per

    def emit(self, reorder=True):
        nc = self.nc
        st = self.stack
        ops = self.ops
        per = self.schedule(reorder)
        for e in ENGS:
            for p_, op in enumerate(per[e]):
                op.pos = p_
        dcnt = {}
        dmas = [(op.start if reorder else 0.0, i) for i, op in enumerate(ops) if op.is_dma]
        dmas.sort()
        for _, i in dmas:
            op = ops[i]
            dcnt[op.dsem] = dcnt.get(op.dsem, 0) + 1
            op.dcount = 16 * dcnt[op.dsem]
        def needs_sem(cons, prod):
            if prod.is_dma:
                return True
            if cons.eng == "pe" and prod.eng == "pe" and not cons.is_dma:
                return False
            return True
        need = []
        for op in ops:
            nd = {}
            for x in op.deps:
                p = ops[x]
                if not needs_sem(op, p):
                    continue
                key = ("d", p.dsem) if p.is_dma else ("e", p.eng)
                cur = nd.get(key)
                if p.is_dma:
                    if cur is None or p.dcount > cur.dcount:
                        nd[key] = p
                else:
                    if cur is None or p.pos > cur.pos:
                        nd[key] = p
            need.append(nd)
            for p in nd.values():
                p.signal = True
        esem = {e: st.enter_context(nc.semaphore("sem_" + e)) for e in ENGS}
        dsems = {k: st.enter_context(nc.semaphore("dsem_" + str(k))) for k in self.dma_sem_counts}
        for e in ENGS:
            c = 0
            for op in per[e]:
                if op.signal and not op.is_dma:
                    c += 1
                    op.count = c
        opidx = {id(op): i for i, op in enumerate(ops)}

        def run(engname, eng):
            waited = {}
            for op in per[engname]:
                nd = need[opidx[id(op)]]
                for key, p in nd.items():
                    val = p.dcount if p.is_dma else p.count
                    if waited.get(key, 0) >= val:
                        continue
                    waited[key] = val
                    eng.wait_ge(dsems[key[1]] if key[0] == "d" else esem[key[1]], val)
                inst = op.fn(eng)
                if op.is_dma:
                    inst.then_inc(dsems[op.dsem], 16)
                elif op.signal:
                    inst.then_inc(esem[op.eng], 1)
            if engname in ("sp", "pool", "act"):
                for k, c in dcnt.items():
                    if str(k).startswith("out") and engname == "sp":
                        eng.wait_ge(dsems[k], 16 * c)

        with nc.Block() as block:
            @block.tensor
            def _(e):
                run("pe", e)

            @block.scalar
            def _(e):
                run("act", e)

            @block.vector
            def _(e):
                run("dve", e)

            @block.gpsimd
            def _(e):
                run("pool", e)

            @block.sync
            def _(e):
                run("sp", e)
        st.close()


FAM = {
    "wada": (1024, 6144, 8, 512),
    "wq": (1024, 1024, 8, 512), "wf": (1024, 1024, 8, 512), "wi": (1024, 1024, 8, 512),
    "wg": (1024, 1024, 8, 512), "wz": (1024, 2048, 8, 512), "wxbc": (1024, 3072, 8, 512),
    "wdt": (1024, 32, 8, 32), "wga": (1024, 1024, 8, 512), "wgb": (1024, 1024, 8, 512),
    "wba": (1024, 1024, 8, 512), "wbb": (2048, 1024, 8, 512), "wo": (1024, 1024, 8, 512),
    "wgate": (1024, D_FF, 8, 256), "wup": (1024, D_FF, 8, 256), "wdown": (D_FF, 1024, 11, 256),
}


def fam_blocks(name):
    K, C, kb, cw = FAM[name]
    return C // cw, (K // 128) // kb


def blockify(W, name):
    K, C, kb, cw = FAM[name]
    ncb, nkg = fam_blocks(name)
    a = np.ascontiguousarray(W, dtype=np.float32).reshape(nkg, kb, 128, ncb, cw)
    a = a.transpose(3, 0, 2, 1, 4)
    return np.ascontiguousarray(a.reshape(ncb * nkg, 128, kb * cw))


REORDER = True
WIN_E = {"pe": 8, "act": 8, "dve": 5, "pool": 1, "sp": 1}


def build(NT=32, dbg=None):
    nc = bass.Bass("TRN2", target_bir_lowering=False)
    P = Prog(nc)
    dram = {}

    def din(name, shape):
        dram[name] = nc.dram_tensor(name, list(shape), F32, kind="ExternalInput").ap()
        return dram[name]

    x_d = din("x", [SEQ, D])
    out_d = nc.dram_tensor("out", [NT * 128, D], F32, kind="ExternalOutput").ap()
    for fn_ in FAM:
        K, C, kb, cw = FAM[fn_]
        ncb, nkg = fam_blocks(fn_)
        din(fn_, [ncb * nkg, 128, kb * cw])
    cT_d = din("cT", [128, 8])
    bada_d = din("bada", [1, 6144])
    lbT_d = din("lbT", [128, 16])
    gnorm_d = din("gnorm", [128, 1])
    convw_d = din("convw", [128, 24 * 4])
    convb_d = din("convb", [128, 24])
    dtb_d = din("dtb", [128, 32])
    alog_d = din("alog", [128, 32])
    dsk_d = din("dsk", [128, 32])
    snorm_d = din("snorm", [128, 16])
    ln_d = {k: din(k, [128, 1024]) for k in ("ln1g", "ln1b", "ln2g", "ln2b")}
    ident_d = din("ident", [128, 128])
    tri_d = din("tri", [128, 128])
    hmask_d = din("hmask", [128, 128])
    negm_d = din("negm", [128, 512])
    smask_d = din("smask", [128, 512])
    dbg_out = {}

    NSLOT = 4
    slots = [P.sb("wslot%d" % i, [128, 4096], BF16) for i in range(NSLOT)]
    slot_ctr = [0]
    NSUB = 2
    TT = NSUB * 128
    xt = [P.sb("xt%d" % i, [128, 1024]) for i in range(2 * NSUB)]
    class Cx:
        pass
    cxA, cxB, cxP = Cx(), Cx(), Cx()
    for cx_, sfx in ((cxA, ""), (cxB, "_b")):
        cx_.xn = P.sb("xn" + sfx, [128, 1024], BF16)
        cx_.uT = P.sb("uT" + sfx, [128, 8, TT], BF16)
        cx_.st6 = P.sb("st6" + sfx, [128, 12])
        cx_.mv = P.sb("mv" + sfx, [128, 2])
        cx_.rstd = P.sb("rstd" + sfx, [128, 1])
        cx_.nmr = P.sb("nmr" + sfx, [128, 1])
        cx_.bctr = 0
    uT = cxA.uT
    CUR = [cxP]
    epsln = P.sb("epsln", [128, 1])
    epsrms = P.sb("epsrms", [128, 1])
    identf = P.sb("identf", [128, 128])
    identb = P.sb("identb", [128, 128], BF16)
    tri = P.sb("tri", [128, 128])
    onesf = P.sb("onesf", [128, 128])
    hmask = P.sb("hmask", [128, 128])
    negmb = P.sb("negmb", [128, 512], BF16)
    smask = P.sb("smask", [128, 512])
    lbraw = P.sb("lbraw", [128, 16])
    lb = P.sb("lb", [128, 8])
    oml = P.sb("oml", [128, 8])
    cTt = P.sb("cTt", [128, 8])
    condb = P.sb("condb", [128, 8], BF16)
    bada = P.sb("bada", [1, 512])
    modrow = P.sb("modrow", [1, 512])
    modc = P.sb("modc", [128, 32])
    lnt = {k: P.sb(k + "_t", [128, 1024]) for k in ln_d}
    gnorm = P.sb("gnorm_t", [128, 1])
    convw = P.sb("convw_t", [128, 24, 4])
    convb = P.sb("convb_t", [128, 24])
    dtb = P.sb("dtb_t", [128, 32])
    negA = P.sb("negA", [128, 32])
    dsk = P.sb("dsk_t", [128, 32])
    snorm = P.sb("snorm_t", [128, 16])
    Sh = P.sb("Sh", [128, 8, 128])
    Shbf = P.sb("Shbf", [128, 8, 128], BF16)
    Sm = P.sb("Sm", [128, 4, 512])
    Smbf = P.sb("Smbf", [128, 4, 512], BF16)
    hist = P.sb("hist", [128, 24, 3])
    sig_h = P.sb("sig_h", [128, 4, TT])
    sq_h = P.sb("sq_h", [128, 4, TT], BF16)
    sg_h = P.sb("sg_h", [128, 4, TT], BF16)
    v_h = P.sb("v_h", [128, NSUB, 512], BF16)
    sga = P.sb("sga", [128, 4, TT], BF16)
    sgb = P.sb("sgb", [128, 4, TT], BF16)
    sz = P.sb("sz", [128, NSUB, 512], BF16)
    xs_tok = P.sb("xs_tok", [128, NSUB, 512], BF16)
    BT = P.sb("BT", [128, 4, TT], BF16)
    CT = P.sb("CT", [128, 4, TT], BF16)
    B_tok = P.sb("B_tok", [128, NSUB, 4, 128], BF16)
    raw = P.sb("raw", [128, 4, TT + 3])
    cacc = P.sb("cacc", [128, 4, TT])
    xc = P.sb("xc", [128, 4, TT], BF16)
    dtp = P.sb("dtp", [128, 32])
    a_tok = P.sb("a_tok", [128, 32])
    delta = P.sb("delta", [128, NSUB, 32])
    acum_tok = P.sb("acum_tok", [128, NSUB, 32])
    nacum_tok = P.sb("nacum_tok", [128, NSUB, 32])
    eacum = P.sb("eacum", [128, NSUB, 32])
    w_tok = P.sb("w_tok", [128, NSUB, 32])
    ealast = P.sb("ealast", [128, NSUB, 32])
    dw = P.sb("dw", [128, NSUB, 32])
    acumT = P.sb("acumT", [32, NSUB, 128])
    rhsx = [P.sb("rhsx0", [32, 4, 128])]
    xdt = P.sb("xdt", [128, 512], BF16)
    xdtw = P.sb("xdtw", [128, 512], BF16)
    decT = [P.sb("decT%d" % i, [128, 4, 128], BF16) for i in range(2)]
    MT = P.sb("MT", [128, 8, 128], BF16)
    t1 = P.sb("t1", [128, 512])
    t2 = P.sb("t2", [128, 512])
    xsD = P.sb("xsD", [128, 512])
    ssq = P.sb("ssq", [128, 1])
    rs_m = P.sb("rs_m", [128, 1])
    yb_tok = P.sb("yb_tok", [128, 512], BF16)
    ybT = P.sb("ybT", [128, 16, TT], BF16)
    ff = P.sb("ff", [128, 4, 128])
    kk = P.sb("kk", [128, 4, 128])
    bc = P.sb("bc", [128, 4, 2, 64])
    dd = P.sb("dd", [128, 4, 2, 64])
    E1 = P.sb("E1", [128, 4, 128], BF16)
    elast = P.sb("elast", [128, 4, 2])
    qh = P.sb("qh", [128, 4, 128], BF16)
    kh = P.sb("kh", [128, 4, 128], BF16)
    kh_tok = P.sb("kh_tok", [128, 4, 128], BF16)
    scm = P.sb("scm", [128, 4, 128], BF16)
    o_sb = ff
    osq = kk
    haT = P.sb("haT", [128, 8, TT], BF16)
    m1 = P.sb("m1", [128, 4, TT])
    m2 = P.sb("m2", [128, 4, TT])
    g1b = T(m1.name, m1[:].rearrange("p a t -> p (a t)"))
    g2b = T(m2.name, m2[:].rearrange("p a t -> p (a t)"))
    mT = P.sb("mT", [128, 8, TT], BF16)
    ftmp = P.sb("ftmp", [128, 2, TT], BF16)
    hmT = P.sb("hmT", [128, 22, TT], BF16)

    banks = [P.ps("bank%d" % i, [128, 512]) for i in range(8)]
    cxP.banks = banks
    cxP.bctr = 0
    cxA.banks = banks[0:6]
    cxB.banks = banks[6:8]

    def nb():
        cx = CUR[0]
        b = cx.banks[cx.bctr % len(cx.banks)]
        cx.bctr += 1
        return b

    def fsz(ap):
        n = 1
        for d_ in ap.shape[1:]:
            n *= int(d_)
        return n

    def ecost(eng, ap):
        f = fsz(ap)
        if eng == "act":
            return 200.0 + 0.6 * f
        if eng == "pool":
            return 300.0 + 7.0 * f
        return 100.0 + 1.04 * f

    def dma(eng, out_ap, in_ap, reads, writes, dsem):
        nb_ = fsz(out_ap) * int(out_ap.shape[0]) * (2 if out_ap.dtype == BF16 else 4)
        P.add(eng, lambda e: e.dma_start(out=out_ap, in_=in_ap), reads=reads, writes=writes, is_dma=True, dsem=dsem, nbytes=nb_)

    def mm(out_ap, lhsT, rhs, start, stop, reads, writes, sgc=False):
        n_ = fsz(rhs)
        if lhsT.dtype == F32:
            c_ = 160.0 + 1.3 * n_
        elif n_ <= 64:
            c_ = 50.0
        elif n_ <= 128:
            c_ = 101.0
        elif n_ <= 256:
            c_ = 131.0
        else:
            c_ = 351.0
        if sgc:
            P.add("pe", lambda e: e.matmul(out_ap, lhsT=lhsT, rhs=rhs, start=start, stop=stop, skip_group_check=True), reads=reads, writes=writes, cost=c_)
        else:
            P.add("pe", lambda e: e.matmul(out_ap, lhsT=lhsT, rhs=rhs, start=start, stop=stop), reads=reads, writes=writes, cost=c_)

    def act(out_ap, in_ap, func, reads, writes, bias=None, scale=None, accum_out=None):
        kw = {}
        if bias is not None:
            kw["bias"] = bias
        if scale is not None:
            kw["scale"] = scale
        if accum_out is not None:
            kw["accum_out"] = accum_out
        tc_ = {AF.Sigmoid: "sig", AF.Silu: "silu", AF.Tanh: "silu", AF.Exp: "el", AF.Ln: "el"}.get(func)
        P.add("act", lambda e: e.activation(out=out_ap, in_=in_ap, func=func, **kw), reads=reads, writes=writes, cost=ecost("act", out_ap), tcls=tc_)

    def tt(out_ap, in0, in1, op, reads, writes, eng="dve"):
        P.add(eng, lambda e: e.tensor_tensor(out=out_ap, in0=in0, in1=in1, op=op), reads=reads, writes=writes, cost=ecost(eng, out_ap))

    def ts(out_ap, in0, s1, s2, op0, op1, reads, writes, eng="dve"):
        c_ = ecost(eng, out_ap)
        if op1 is None:
            P.add(eng, lambda e: e.tensor_scalar(out=out_ap, in0=in0, scalar1=s1, scalar2=None, op0=op0), reads=reads, writes=writes, cost=c_)
        else:
            P.add(eng, lambda e: e.tensor_scalar(out=out_ap, in0=in0, scalar1=s1, scalar2=s2, op0=op0, op1=op1), reads=reads, writes=writes, cost=c_)

    def stt(out_ap, in0, scalar, in1, op0, op1, reads, writes, eng="dve"):
        P.add(eng, lambda e: e.scalar_tensor_tensor(out=out_ap, in0=in0, scalar=scalar, in1=in1, op0=op0, op1=op1), reads=reads, writes=writes,
              cost=ecost(eng, out_ap))

    def cp(out_ap, in_ap, reads, writes, eng="dve"):
        c_ = ecost(eng, out_ap)
        if eng == "act":
            P.add("act", lambda e: e.copy(out=out_ap, in_=in_ap), reads=reads, writes=writes, cost=c_)
        else:
            P.add(eng, lambda e: e.tensor_copy(out=out_ap, in_=in_ap), reads=reads, writes=writes, cost=c_)

    def memset(t, val, eng="dve"):
        P.add(eng, lambda e: e.memset(t[:], val), writes=[t])

    scratch = {}
    for fam_ in FAM:
        if fam_ == "wada":
            continue
        K_, C_, kb_, cw_ = FAM[fam_]
        ncb_, nkg_ = fam_blocks(fam_)
        scratch[fam_] = nc.dram_tensor("ws_" + fam_, [ncb_ * nkg_, 128, kb_ * cw_], BF16, kind="Internal").ap()

    converted = set()

    def wload(fam, blk, from_f32=False):
        K, C, kb, cw = FAM[fam]
        i = slot_ctr[0] % NSLOT
        slot_ctr[0] += 1
        s = slots[i]
        if from_f32:
            dma("pool", s[:, 0:kb * cw], dram[fam][blk], reads=[], writes=[s], dsem="wc%d" % i)
        elif (fam, blk) not in converted:
            converted.add((fam, blk))
            dma("pool", s[:, 0:kb * cw], dram[fam][blk], reads=[], writes=[s], dsem="wc%d" % i)
            ncb, nkg = fam_blocks(fam)
            cbi, kgi = blk // nkg, blk % nkg
            s3 = s[:, 0:kb * cw].rearrange("p (k c) -> p k c", k=kb)
            if fam == "wo":
                tt(s3, s3, bcast(g1b[:, cbi * cw:(cbi + 1) * cw].unsqueeze(1), [128, kb, cw]), ALU.mult, [s, g1b], [s])
            elif fam == "wdown":
                tt(s3, s3, bcast(g2b[:, cbi * cw:(cbi + 1) * cw].unsqueeze(1), [128, kb, cw]), ALU.mult, [s, g2b], [s])
            elif fam == "wbb":
                tt(s3, s3, bcast(snorm[:, kgi * kb:(kgi + 1) * kb].unsqueeze(2), [128, kb, cw]), ALU.mult, [s, snorm], [s])
            dma("sp", scratch[fam][blk], s[:, 0:kb * cw], reads=[s], writes=[("ws_" + fam, blk)], dsem="wst%d" % i)
        else:
            dma("sp", s[:, 0:kb * cw], scratch[fam][blk], reads=[("ws_" + fam, blk)], writes=[s], dsem="w%d" % i)
        return s

    def wv(s, fam, k, c0, n):
        K, C, kb, cw = FAM[fam]
        return s[:, k * cw + c0: k * cw + c0 + n]

    def bcast(ap, shape):
        return ap.to_broadcast(list(shape))

    cq = ["sp"]

    def cload(t, d, name):
        dma("sp", t[:], d, reads=[], writes=[t], dsem="c_" + name)

    cload(identf, ident_d, "ident")
    cload(tri, tri_d, "tri")
    cload(hmask, hmask_d, "hmask")
    dma("pool", negmb[:], negm_d, reads=[], writes=[negmb], dsem="c_negm")
    cload(smask, smask_d, "smask")
    cload(lbraw, lbT_d, "lb")
    cload(cTt, cT_d, "cT")
    cload(gnorm, gnorm_d, "gnorm")
    dma("sp", convw[:].rearrange("p a b -> p (a b)"), convw_d, reads=[], writes=[convw], dsem="c_convw")
    cload(convb, convb_d, "convb")
    cload(dtb, dtb_d, "dtb")
    cload(negA, alog_d, "alog")
    cload(dsk, dsk_d, "dsk")
    cload(snorm, snorm_d, "snorm")
    for k_ in ln_d:
        cload(lnt[k_], ln_d[k_], k_)
    memset(onesf, 1.0)
    memset(epsln, LN_EPS)
    memset(epsrms, RMS_EPS)
    memset(Sh, 0.0)
    memset(Sm, 0.0)
    memset(Smbf, 0.0)
    memset(hist, 0.0)
    cp(identb[:], identf[:], [identf], [identb])
    act(negA[:], negA[:], AF.Exp, [negA], [negA])
    ts(negA[:], negA[:], -1.0, None, ALU.mult, None, [negA], [negA])
    tt(lb[:], lbraw[:, 0:8], lbraw[:, 8:16], ALU.subtract, [lbraw], [lb])
    act(lb[:], lb[:], AF.Sigmoid, [lb], [lb])
    ts(oml[:], lb[:], -0.5, 0.5, ALU.mult, ALU.add, [lb], [oml])
    tt(lb[:], lb[:], oml[:], ALU.add, [lb, oml], [lb])
    act(condb[:], cTt[:], AF.Silu, [cTt], [condb])
    for cb in range(12):
        s = wload("wada", cb, from_f32=True)
        b = nb()
        for k in range(8):
            mm(b[0:1, 0:512], condb[:, k:k + 1], wv(s, "wada", k, 0, 512), k == 0, k == 7, [condb, s], [b])
        dma("sp", bada[:], bada_d[0:1, cb * 512:(cb + 1) * 512], reads=[], writes=[bada], dsem="c_bada")
        tt(modrow[0:1, :], b[0:1, 0:512], bada[0:1, :], ALU.add, [b, bada], [modrow])
        sec, half = cb // 2, cb % 2
        if sec in (2, 5):
            gt = g1b if sec == 2 else g2b
            b2 = nb()
            mm(b2[:, 0:512], onesf[0:1, 0:128], modrow[0:1, :], True, True, [onesf, modrow], [b2])
            ts(gt[:, half * 512:(half + 1) * 512], b2[:, 0:512], (0.5 if sec == 2 else 1.0), None, ALU.mult, None, [b2], [(gt, half)])
        else:
            vi = {0: 0, 1: 1, 3: 2, 4: 3}[sec]
            b2 = nb()
            for j in range(4):
                mm(b2[:, j:j + 1], modrow[0:1, j * 128:(j + 1) * 128], onesf[0:1, 0:1], True, True, [modrow, onesf], [b2])
            c0 = vi * 8 + half * 4
            cp(modc[:, c0:c0 + 4], b2[:, 0:4], [b2], [(modc, (vi, half))])
            if vi in (1, 3) and half == 1:
                ts(modc[:, vi * 8:(vi + 1) * 8], modc[:, vi * 8:(vi + 1) * 8], 1.0, None, ALU.add, None, [(modc, (vi, 0)), (modc, (vi, 1))], [(modc, (vi, 0)), (modc, (vi, 1))])

    for blk_ in range(2):
        wload("wo", blk_)
    for blk_ in range(8):
        wload("wdown", blk_)

    def layernorm_stats(src):
        st6, mv, rstd = CUR[0].st6, CUR[0].mv, CUR[0].rstd
        P.add("dve", lambda e: e.bn_stats(out=st6[:, 0:6], in_=src[:, 0:512]), reads=[src], writes=[(st6, 0)], cost=650.0)
        P.add("dve", lambda e: e.bn_stats(out=st6[:, 6:12], in_=src[:, 512:1024]), reads=[src], writes=[(st6, 1)], cost=650.0)
        P.add("dve", lambda e: e.bn_aggr(out=mv[:], in_=st6[:]), reads=[st6], writes=[mv])
        act(rstd[:], mv[:, 1:2], AF.Ln, [mv, epsln], [rstd], bias=epsln[:, 0:1], scale=1.0)
        act(rstd[:], rstd[:], AF.Exp, [rstd], [rstd], scale=-0.5)

    def make_uT(src, sc_off, sh_off, sub):
        cx = CUR[0]
        mv, rstd, nmr, xn, uT = cx.mv, cx.rstd, cx.nmr, cx.xn, cx.uT
        layernorm_stats(src)
        stt(nmr[:], mv[:, 0:1], -1.0, rstd[:], ALU.mult, ALU.mult, [mv, rstd], [nmr])
        act(xn[:], src[:], AF.Identity, [src, rstd, nmr], [xn], bias=nmr[:, 0:1], scale=rstd[:, 0:1])
        for half in range(2):
            b = nb()
            for c in range(4):
                k = half * 4 + c
                mm(b[:, c * 128:(c + 1) * 128], xn[:, k * 128:(k + 1) * 128], identb[:], True, True, [xn, identb], [b])
            for c in range(4):
                k = half * 4 + c
                act(uT[:, k, sub * 128:(sub + 1) * 128], b[:, c * 128:(c + 1) * 128], AF.Identity, [b] + [(modc, (v_, h_)) for v_ in (sh_off // 8, sc_off // 8) for h_ in (0, 1)], [(uT, (k, sub))],
                    bias=modc[:, sh_off + k: sh_off + k + 1], scale=modc[:, sc_off + k: sc_off + k + 1])

    def proj_fm(fam, blk, ntiles=4):
        s = wload(fam, blk)
        uT = CUR[0].uT
        bks = [nb() for _ in range((ntiles + 1) // 2)]
        for c in range(ntiles):
            b = bks[c // 2]
            for k in range(8):
                mm(b[:, (c % 2) * TT:(c % 2 + 1) * TT], wv(s, fam, k, c * 128, 128), uT[:, k, :], k == 0, k == 7, [s, uT], [b])
        return bks

    def b3(bank):
        return bank[:].rearrange("p (a t) -> p a t", a=2)

    def proj_tm_sub(s, fam, sub, ncols):
        uT = CUR[0].uT
        b = nb()
        for k in range(8):
            mm(b[:, 0:ncols], uT[:, k, sub * 128:(sub + 1) * 128], wv(s, fam, k, 0, ncols), k == 0, k == 7, [s, uT], [b])
        return b

    def dbg_tap(name, tile_ap, shape, rd):
        if dbg is not None and name in dbg:
            o = nc.dram_tensor("dbg_" + name, list(shape), tile_ap.dtype, kind="ExternalOutput").ap()
            dbg_out[name] = o
            dma("sp", o, tile_ap, reads=rd, writes=[], dsem="out_dbg_" + name)

    NST = NT // NSUB
    assert NT % NSUB == 0

    def xbuf(st, sub):
        return xt[(st % 2) * NSUB + sub]

    def load_x(st):
        for sub in range(NSUB):
            Xn = xbuf(st, sub)
            ti = st * NSUB + sub
            dma("pool", Xn[:], x_d[ti * 128:(ti + 1) * 128, :], reads=[], writes=[Xn], dsem="x%d" % ((st % 2) * NSUB + sub))

    def conv_block(blk, XC):
        bks = proj_fm("wxbc", blk)
        for hb in range(2):
            cp(raw[:, hb * 2:(hb + 1) * 2, 3:3 + TT], b3(bks[hb]), [bks[hb]], [(raw, ("m", hb))], eng="act")
        cp(raw[:, :, 0:3], hist[:, blk * 4:(blk + 1) * 4, :], [(hist, blk)], [(raw, "h")], eng="pool")
        cp(hist[:, blk * 4:(blk + 1) * 4, :], raw[:, :, TT:TT + 3], [(raw, ("m", 0)), (raw, ("m", 1))], [(hist, blk)], eng="pool")
        for kk_ in range(4):
            for c in range(4):
                j = blk * 4 + c
                if kk_ == 0:
                    ts(cacc[:, c, :], raw[:, c, 0:TT], convw[:, j, 0:1], None, ALU.mult, None, [raw, convw], [(cacc, c)])
                else:
                    stt(cacc[:, c, :], raw[:, c, kk_:kk_ + TT], convw[:, j, kk_:kk_ + 1], cacc[:, c, :], ALU.mult, ALU.add,
                        [raw, convw, (cacc, c)], [(cacc, c)])
        for c in range(4):
            j = blk * 4 + c
            act(XC[:, c, :], cacc[:, c, :], AF.Silu, [(cacc, c), convb], [(XC, c)], bias=convb[:, j:j + 1], scale=1.0)

    def dt_chain(s, sub):
        b = proj_tm_sub(s, "wdt", sub, 32)
        tt(dtp[:], b[:, 0:32], dtb[:], ALU.add, [b, dtb], [dtp])
        act(delta[:, sub, :], dtp[:], AF.Exp, [dtp], [(delta, sub)])
        act(delta[:, sub, :], delta[:, sub, :], AF.Ln, [(delta, sub)], [(delta, sub)], bias=1.0, scale=1.0)
        tt(a_tok[:], delta[:, sub, :], negA[:], ALU.mult, [(delta, sub), negA], [a_tok])
        p1 = nb()
        mm(p1[:, 0:32], tri[:], a_tok[:], True, True, [tri, a_tok], [p1])
        mm(p1[:, 32:64], onesf[:], a_tok[:], True, True, [onesf, a_tok], [p1])
        mm(p1[0:32, 64:192], a_tok[:], tri[:], True, True, [tri, a_tok], [p1])
        cp(acum_tok[:, sub, :], p1[:, 0:32], [p1], [(acum_tok, sub)])
        ts(nacum_tok[:, sub, :], acum_tok[:, sub, :], -1.0, None, ALU.mult, None, [(acum_tok, sub)], [(nacum_tok, sub)])
        act(eacum[:, sub, :], p1[:, 0:32], AF.Exp, [p1], [(eacum, sub)])
        tt(w_tok[:, sub, :], p1[:, 32:64], acum_tok[:, sub, :], ALU.subtract, [p1, (acum_tok, sub)], [(w_tok, sub)])
        act(w_tok[:, sub, :], w_tok[:, sub, :], AF.Exp, [(w_tok, sub)], [(w_tok, sub)])
        act(ealast[:, sub, :], p1[:, 32:64], AF.Exp, [p1], [(ealast, sub)])
        tt(dw[:, sub, :], delta[:, sub, :], w_tok[:, sub, :], ALU.mult, [(delta, sub), (w_tok, sub)], [(dw, sub)])
        cp(acumT[:, sub, :], p1[0:32, 64:192], [p1], [(acumT, sub)])

    def mamba_core(sub, g):
        hs = slice(g * 8, (g + 1) * 8)
        tc_ = slice(sub * 128, (sub + 1) * 128)
        dl_b = bcast(delta[:, sub, hs].unsqueeze(2), [128, 8, 64])
        dw_b = bcast(dw[:, sub, hs].unsqueeze(2), [128, 8, 64])
        xs3 = xs_tok[:, sub, :].rearrange("p (h q) -> p h q", h=8)
        tt(xdt[:].rearrange("p (h q) -> p h q", h=8), xs3, dl_b, ALU.mult, [(xs_tok, sub), (delta, sub)], [xdt])
        tt(xdtw[:].rearrange("p (h q) -> p h q", h=8), xs3, dw_b, ALU.mult, [(xs_tok, sub), (dw, sub)], [xdtw])
        bcb = nb()
        mm(bcb[:, 0:128], BT[:, g, tc_], CT[:, g, tc_], True, True, [BT, CT], [bcb])
        R = rhsx[0]
        for hh in range(2):
            h0 = g * 8 + hh * 4
            tt(R[:], bcast(acumT[:, sub, :].unsqueeze(1), [32, 4, 128]), bcast(identf[0:32, h0:h0 + 4].unsqueeze(2), [32, 4, 128]),
               ALU.mult, [(acumT, sub), identf], [R])
            bd = nb()
            DT_ = decT[hh]
            mm(bd[:, 0:512], onesf[0:32, :], R[:].rearrange("p a t -> p (a t)"), True, False, [onesf, R], [bd])
            mm(bd[:, 0:512], identb[:], negmb[:], False, True, [identb, negmb], [bd])
            for hl in range(4):
                h = g * 8 + hh * 4 + hl
                act(DT_[:, hl, :], bd[:, hl * 128:(hl + 1) * 128], AF.Exp, [bd, (nacum_tok, sub)], [(DT_, hl)],
                    bias=nacum_tok[:, sub, h:h + 1], scale=1.0)
            tt(MT[:, hh * 4:(hh + 1) * 4, :], DT_[:], bcast(bcb[:, 0:128].unsqueeze(1), [128, 4, 128]), ALU.mult,
               [DT_, bcb], [(MT, hh)])
        yield
        by = nb()
        for h in range(8):
            mm(by[:, h * 64:(h + 1) * 64], MT[:, h, :], xdt[:, h * 64:(h + 1) * 64], True, True, [MT, xdt], [by])
        bi = nb()
        mm(bi[:, 0:512], CT[:, g, tc_], Smbf[:, g, :], True, True, [CT, (Smbf, g)], [bi])
        ea_b = bcast(eacum[:, sub, hs].unsqueeze(2), [128, 8, 64])
        tt(t1[:].rearrange("p (h q) -> p h q", h=8), bi[:].rearrange("p (h q) -> p h q", h=8), ea_b, ALU.mult, [bi, (eacum, sub)], [t1])
        tt(t2[:], t1[:], by[:, 0:512], ALU.add, [t1, by], [t2])
        tt(xsD[:].rearrange("p (h q) -> p h q", h=8), xs3, bcast(dsk[:, hs].unsqueeze(2), [128, 8, 64]), ALU.mult,
           [(xs_tok, sub), dsk], [xsD])
        tt(t2[:], t2[:], xsD[:], ALU.add, [t2, xsD], [t2])
        tt(t2[:], t2[:], sz[:, sub, :], ALU.mult, [t2, (sz, sub)], [t2])
        act(t1[:], t2[:], AF.Square, [t2], [t1, ssq], accum_out=ssq[:, 0:1])
        act(rs_m[:], ssq[:], AF.Ln, [ssq, epsrms], [rs_m], bias=epsrms[:, 0:1], scale=1.0 / 512.0)
        act(rs_m[:], rs_m[:], AF.Exp, [rs_m], [rs_m], scale=-0.5)
        ts(yb_tok[:], t2[:], rs_m[:, 0:1], None, ALU.mult, None, [t2, rs_m], [yb_tok])
        bt_ = nb()
        for c in range(4):
            mm(bt_[:, c * 128:(c + 1) * 128], yb_tok[:, c * 128:(c + 1) * 128], identb[:], True, True, [yb_tok, identb], [bt_])
        cp(ybT[:, g * 4:(g + 1) * 4, tc_], bt_[:].rearrange("p (a t) -> p a t", a=4), [bt_], [(ybT, (g, sub))], eng="act")
        bs = nb()
        mm(bs[:, 0:512], B_tok[:, sub, g, :], xdtw[:], True, True, [B_tok, xdtw], [bs])
        tt(Sm[:, g, :].rearrange("p (h q) -> p h q", h=8), Sm[:, g, :].rearrange("p (h q) -> p h q", h=8),
           bcast(ealast[:, sub, hs].unsqueeze(2), [128, 8, 64]), ALU.mult, [(Sm, g), (ealast, sub)], [(Sm, g)])
        tt(Sm[:, g, :], Sm[:, g, :], bs[:, 0:512], ALU.add, [(Sm, g), bs], [(Sm, g)])
        cp(Smbf[:, g, :], Sm[:, g, :], [(Sm, g)], [(Smbf, g)], eng="act")
        yield

    def hgrn_core(sub, half):
        hs = slice(half * 4, (half + 1) * 4)
        tc_ = slice(sub * 128, (sub + 1) * 128)
        tt(ff[:], sig_h[:, :, tc_], bcast(oml[:, hs].unsqueeze(2), [128, 4, 128]), ALU.mult, [sig_h, oml], [ff])
        tt(ff[:], ff[:], bcast(lb[:, hs].unsqueeze(2), [128, 4, 128]), ALU.add, [ff, lb], [ff])
        act(kk[:], ff[:], AF.Identity, [ff], [kk], bias=1.0, scale=-1.0)
        act(ff[:], ff[:], AF.Ln, [ff], [ff])
        bcf = bc[:].rearrange("p a b c -> p (a b c)")
        P.add("dve", lambda e: e.tensor_tensor_scan(out=bcf, data0=smask[:], data1=ff[:].rearrange("p a t -> p (a t)"),
                                                    initial=0.0, op0=ALU.mult, op1=ALU.add), reads=[smask, ff], writes=[bc], cost=650.0)
        tt(dd[:], bc[:], bcast(bc[:, :, :, 63:64], [128, 4, 2, 64]), ALU.subtract, [bc], [dd])
        ddf = dd[:].rearrange("p a b c -> p a (b c)")
        act(E1[:], ddf, AF.Exp, [dd], [E1])
        act(ddf, ddf, AF.Exp, [dd], [dd], scale=-1.0)
        act(elast[:].unsqueeze(3), bc[:, :, :, 63:64], AF.Exp, [bc], [elast])
        stt(qh[:], sq_h[:, :, tc_], 128.0 ** -0.5, E1[:], ALU.mult, ALU.mult, [sq_h, E1], [qh])
        tt(kh[:], kk[:], ddf, ALU.mult, [kk, dd], [kh])
        yield
        b = nb()
        for c in range(4):
            mm(b[:, c * 128:(c + 1) * 128], kh[:, c, :], identb[:], True, True, [kh, identb], [b])
        cp(kh_tok[:], b[:].rearrange("p (a t) -> p a t", a=4), [b], [kh_tok], eng="act")
        b = nb()
        for c in range(4):
            mm(b[:, c * 128:(c + 1) * 128], kh[:, c, :], qh[:, c, :], True, True, [kh, qh], [b])
        tt(scm[:], b[:].rearrange("p (a t) -> p a t", a=4), bcast(hmask[:].unsqueeze(1), [128, 4, 128]), ALU.mult, [b, hmask], [scm])
        ob = nb()
        dsb = nb()
        for hl in range(4):
            mm(ob[:, hl * 128:(hl + 1) * 128], v_h[:, sub, hl * 128:(hl + 1) * 128], scm[:, hl, :], hl == 0, False,
               [(v_h, sub), scm], [ob], sgc=True)
        for c in range(2):
            tt(Sh[:, hs, :], Sh[:, hs, :], bcast(elast[:, :, c:c + 1], [128, 4, 128]), ALU.mult, [(Sh, half), elast], [(Sh, half)])
            cp(Shbf[:, hs, :], Sh[:, hs, :], [(Sh, half)], [(Shbf, half)], eng="act")
            for hl in range(4):
                mm(ob[:, hl * 128 + c * 64: hl * 128 + (c + 1) * 64], Shbf[:, half * 4 + hl, :], qh[:, hl, c * 64:(c + 1) * 64],
                   False, c == 1, [(Shbf, half), qh], [ob], sgc=True)
            for hl in range(4):
                mm(dsb[:, hl * 128:(hl + 1) * 128], kh_tok[c * 64:(c + 1) * 64, hl, :],
                   v_h[c * 64:(c + 1) * 64, sub, hl * 128:(hl + 1) * 128], True, True, [kh_tok, (v_h, sub)], [dsb])
            tt(Sh[:, hs, :], Sh[:, hs, :], dsb[:].rearrange("p (a t) -> p a t", a=4), ALU.add, [(Sh, half), dsb], [(Sh, half)])
        yield
        cp(o_sb[:], ob[:].rearrange("p (a t) -> p a t", a=4), [ob], [o_sb], eng="act")
        act(osq[:], o_sb[:], AF.Square, [o_sb], [osq])
        b = nb()
        mm(b[:, 0:512], onesf[:], osq[:].rearrange("p a t -> p (a t)"), True, True, [onesf, osq], [b])
        act(osq[:], b[:].rearrange("p (a t) -> p a t", a=4), AF.Ln, [b, epsrms], [osq], bias=epsrms[:, 0:1], scale=1.0 / 128.0)
        act(osq[:], osq[:], AF.Exp, [osq], [osq], scale=-0.5)
        tt(o_sb[:], o_sb[:], osq[:], ALU.mult, [o_sb, osq], [o_sb])
        stt(haT[:, hs, tc_], o_sb[:], gnorm[:, 0:1], sg_h[:, :, tc_], ALU.mult, ALU.mult, [o_sb, gnorm, sg_h], [(haT, (half, sub))])
        yield

    def genA(st):
        Xs = [xbuf(st, sub) for sub in range(NSUB)]
        mv, rstd = cxA.mv, cxA.rstd
        for sub in range(NSUB):
            make_uT(Xs[sub], 8, 0, sub)
            yield
        if st == 0:
            dbg_tap("uT", uT[:], [128, 8, TT], [uT])
        s = wload("wdt", 0)
        for sub in range(NSUB):
            dt_chain(s, sub)
            yield
        conv_block(4, BT)
        for sub in range(NSUB):
            b2 = nb()
            for c in range(4):
                mm(b2[:, c * 128:(c + 1) * 128], BT[:, c, sub * 128:(sub + 1) * 128], identb[:], True, True, [BT, identb], [b2])
            cp(B_tok[:, sub, :, :], b2[:].rearrange("p (a t) -> p a t", a=4), [b2], [(B_tok, sub)], eng="act")
        yield
        conv_block(5, CT)
        yield
        for g in range(4):
            conv_block(g, xc)
            for sub in range(NSUB):
                b2 = nb()
                for c in range(4):
                    mm(b2[:, c * 128:(c + 1) * 128], xc[:, c, sub * 128:(sub + 1) * 128], identb[:], True, True, [xc, identb], [b2])
                cp(xs_tok[:, sub, :], b2[:, 0:512], [b2], [(xs_tok, sub)], eng="act")
            yield
            s = wload("wz", g)
            for sub in range(NSUB):
                b = proj_tm_sub(s, "wz", sub, 512)
                act(sz[:, sub, :], b[:, 0:512], AF.Silu, [b], [(sz, sub)])
            yield
            for sub in range(NSUB):
                for _ in mamba_core(sub, g):
                    yield
        if st == 0:
            dbg_tap("ybT", ybT[:], [128, 16, TT], [ybT])
        for half in range(2):
            bks = proj_fm("wf", half)
            for hb in range(2):
                act(sig_h[:, hb * 2:(hb + 1) * 2, :], b3(bks[hb]), AF.Tanh, [bks[hb]], [(sig_h, hb)], scale=0.5)
            yield
            for (fam, dst) in (("wq", sq_h), ("wg", sg_h)):
                bks = proj_fm(fam, half)
                for hb in range(2):
                    act(dst[:, hb * 2:(hb + 1) * 2, :], b3(bks[hb]), AF.Silu, [bks[hb]], [(dst, hb)])
                yield
            s = wload("wi", half)
            for sub in range(NSUB):
                b = proj_tm_sub(s, "wi", sub, 512)
                cp(v_h[:, sub, :], b[:, 0:512], [b], [(v_h, sub)], eng="act")
            yield
            for sub in range(NSUB):
                for _ in hgrn_core(sub, half):
                    yield
        if st == 0:
            dbg_tap("haT", haT[:], [128, 8, TT], [haT])
        for blk in range(2):
            for (fam, dst) in (("wga", sga), ("wgb", sgb)):
                bks = proj_fm(fam, blk)
                for hb in range(2):
                    act(dst[:, hb * 2:(hb + 1) * 2, :], b3(bks[hb]), AF.Tanh, [bks[hb]], [(dst, hb)], scale=0.5)
                yield
            s = wload("wba", blk)
            bks = [nb(), nb()]
            for c in range(4):
                b = bks[c // 2]
                for k in range(8):
                    mm(b[:, (c % 2) * TT:(c % 2 + 1) * TT], wv(s, "wba", k, c * 128, 128), haT[:, k, :], k == 0, k == 7, [s, haT], [b])
            for hb in range(2):
                stt(m1[:, hb * 2:(hb + 1) * 2, :], sga[:, hb * 2:(hb + 1) * 2, :], 1.0, b3(bks[hb]), ALU.add, ALU.mult, [bks[hb], sga], [(m1, hb)])
            yield
            s0 = wload("wbb", blk * 2 + 0)
            s1 = wload("wbb", blk * 2 + 1)
            bks = [nb(), nb()]
            for c in range(4):
                b = bks[c // 2]
                for kg, s_ in enumerate((s0, s1)):
                    for k in range(8):
                        mm(b[:, (c % 2) * TT:(c % 2 + 1) * TT], wv(s_, "wbb", k, c * 128, 128), ybT[:, kg * 8 + k, :],
                           kg == 0 and k == 0, kg == 1 and k == 7, [s_, ybT], [b])
            for hb in range(2):
                stt(m2[:, hb * 2:(hb + 1) * 2, :], sgb[:, hb * 2:(hb + 1) * 2, :], 1.0, b3(bks[hb]), ALU.add, ALU.mult, [bks[hb], sgb], [(m2, hb)])
            tt(mT[:, blk * 4:(blk + 1) * 4, :], m1[:], m2[:], ALU.add, [m1, m2], [(mT, blk)])
            yield
        if st == 0:
            dbg_tap("mT", mT[:], [128, 8, TT], [mT])
        for cb in range(2):
            s = wload("wo", cb)
            for sub in range(NSUB):
                X = Xs[sub]
                b = nb()
                for k in range(8):
                    mm(b[:, 0:512], mT[:, k, sub * 128:(sub + 1) * 128], wv(s, "wo", k, 0, 512), k == 0, k == 7, [s, mT], [b])
                stt(X[:, cb * 512:(cb + 1) * 512], X[:, cb * 512:(cb + 1) * 512], ALPHA, b[:, 0:512], ALU.mult, ALU.add, [X, b], [X])
            yield
        for sub in range(NSUB):
            X = Xs[sub]
            layernorm_stats(X)
            stt(cxA.nmr[:], mv[:, 0:1], -1.0, rstd[:], ALU.mult, ALU.mult, [mv, rstd], [cxA.nmr])
            act(X[:], X[:], AF.Identity, [X, rstd, cxA.nmr], [X], bias=cxA.nmr[:, 0:1], scale=rstd[:, 0:1])
            tt(X[:], X[:], lnt["ln1g"][:], ALU.mult, [X, lnt["ln1g"]], [X])
            tt(X[:], X[:], lnt["ln1b"][:], ALU.add, [X, lnt["ln1b"]], [X])
            if st == 0 and sub == 0:
                dbg_tap("x1", X[:], [128, 1024], [X])
            yield

    def genB(st):
        Xs = [xbuf(st, sub) for sub in range(NSUB)]
        uTb = cxB.uT
        mv, rstd = cxB.mv, cxB.rstd
        for sub in range(NSUB):
            make_uT(Xs[sub], 24, 16, sub)
            yield
        for blk in range(11):
            sg_ = wload("wgate", blk)
            su_ = wload("wup", blk)
            bg = nb()
            bu = nb()
            for c in range(2):
                for k in range(8):
                    mm(bg[:, c * TT:(c + 1) * TT], wv(sg_, "wgate", k, c * 128, 128), uTb[:, k, :], k == 0, k == 7, [sg_, uTb], [bg])
            for c in range(2):
                for k in range(8):
                    mm(bu[:, c * TT:(c + 1) * TT], wv(su_, "wup", k, c * 128, 128), uTb[:, k, :], k == 0, k == 7, [su_, uTb], [bu])
            act(ftmp[:], b3(bg), AF.Silu, [bg], [ftmp])
            tt(hmT[:, blk * 2:(blk + 1) * 2, :], ftmp[:], b3(bu), ALU.mult, [ftmp, bu], [(hmT, blk)])
            yield
        for cb in range(4):
            s0 = wload("wdown", cb * 2 + 0)
            s1 = wload("wdown", cb * 2 + 1)
            for sub in range(NSUB):
                X = Xs[sub]
                b = nb()
                for kg, s_ in enumerate((s0, s1)):
                    for k in range(11):
                        mm(b[:, 0:256], hmT[:, kg * 11 + k, sub * 128:(sub + 1) * 128], wv(s_, "wdown", k, 0, 256),
                           kg == 0 and k == 0, kg == 1 and k == 10, [s_, hmT], [b])
                stt(X[:, cb * 256:(cb + 1) * 256], X[:, cb * 256:(cb + 1) * 256], ALPHA, b[:, 0:256], ALU.mult, ALU.add, [X, b], [X])
            yield
        for sub in range(NSUB):
            X = Xs[sub]
            ti = st * NSUB + sub
            layernorm_stats(X)
            stt(cxB.nmr[:], mv[:, 0:1], -1.0, rstd[:], ALU.mult, ALU.mult, [mv, rstd], [cxB.nmr])
            act(X[:], X[:], AF.Identity, [X, rstd, cxB.nmr], [X], bias=cxB.nmr[:, 0:1], scale=rstd[:, 0:1])
            tt(X[:], X[:], lnt["ln2g"][:], ALU.mult, [X, lnt["ln2g"]], [X])
            tt(X[:], X[:], lnt["ln2b"][:], ALU.add, [X, lnt["ln2b"]], [X])
            dma("pool", out_d[ti * 128:(ti + 1) * 128, :], X[:], reads=[X], writes=[], dsem="out%d" % ((st % 2) * NSUB + sub))
            yield

    def run_all(gen, cx):
        n = 0
        CUR[0] = cx
        for _ in gen:
            n += 1
            CUR[0] = cx
        return n

    load_x(0)
    if NST > 1:
        load_x(1)
    nA = run_all(genA(0), cxA)
    NB_UNITS = 2 * NSUB + 15
    for st in range(NST):
        gb = genB(st)
        ga = genA(st + 1) if st + 1 < NST else None
        a_done = 0
        for i in range(NB_UNITS):
            CUR[0] = cxB
            next(gb)
            if ga is not None:
                target = ((i + 1) * nA) // NB_UNITS
                while a_done < target:
                    CUR[0] = cxA
                    next(ga)
                    a_done += 1
        for _ in gb:
            raise AssertionError("genB unit count mismatch")
        if ga is not None:
            for _ in ga:
                raise AssertionError("genA unit count mismatch")
        if st + 2 < NST:
            load_x(st + 2)

    P.emit(reorder=REORDER)
    if dbg is not None:
        print("est_ms", getattr(P, "est_ns", 0) / 1e6, flush=True)
    return nc, dbg_out


def host_consts():
    s = np.arange(128)[:, None]
    t = np.arange(128)[None, :]
    tri = (s <= t).astype(np.float32)
    hmask = ((s <= t) & ((s // 64) == (t // 64))).astype(np.float32)
    negm = np.where(s <= t, 0.0, -30000.0).astype(np.float32)
    negm4 = np.ascontiguousarray(np.tile(negm, (1, 4)))
    sm = np.ones((128, 512), np.float32)
    sm[:, ::64] = 0.0
    return {"ident": np.eye(128, dtype=np.float32), "tri": tri, "hmask": hmask, "negm": negm4, "smask": sm}


def host_shared(inp):
    f = lambda a: np.ascontiguousarray(np.asarray(a, dtype=np.float32))
    sh = {}
    w_in = f(inp["w_in"])[0]
    offs = np.cumsum([0, 1024, 1024, 1024, 1024, 2048, 3072, 32, 1024, 1024])
    names = ["wq", "wf", "wi", "wg", "wz", "wxbc", "wdt", "wga", "wgb"]
    for i, n in enumerate(names):
        sh[n] = blockify(w_in[:, offs[i]:offs[i + 1]], n)
    sh["wada"] = blockify(f(inp["w_ada"])[0], "wada")
    sh["wba"] = blockify(f(inp["w_branch_a"])[0], "wba")
    sh["wbb"] = blockify(f(inp["w_branch_b"])[0], "wbb")
    sh["wo"] = blockify(f(inp["w_o"])[0], "wo")
    sh["wgate"] = blockify(f(inp["w_ffn_gate"])[0], "wgate")
    sh["wup"] = blockify(f(inp["w_ffn_up"])[0], "wup")
    sh["wdown"] = blockify(f(inp["w_ffn_down"])[0], "wdown")
    sh["bada"] = f(inp["b_ada"]).reshape(1, 6144)
    lbr = f(inp["hgrn_lb"])
    sh["lbT"] = np.ascontiguousarray(lbr.reshape(2, 8, 128).transpose(2, 0, 1).reshape(128, 16))
    sh["gnorm"] = f(inp["hgrn_gnorm"])[0].reshape(128, 1)
    cw = f(inp["ssm_conv_w"])[0]
    sh["convw"] = np.ascontiguousarray(cw.reshape(4, 24, 128).transpose(2, 1, 0).reshape(128, 96))
    sh["convb"] = np.ascontiguousarray(f(inp["ssm_conv_b"])[0].reshape(24, 128).T)
    rep = lambda v, n: np.ascontiguousarray(np.broadcast_to(f(v).reshape(1, n), (128, n)))
    sh["dtb"] = rep(inp["ssm_dt_bias"][0], 32)
    sh["alog"] = rep(inp["ssm_a_log"][0], 32)
    sh["dsk"] = rep(inp["ssm_d"][0], 32)
    sh["snorm"] = np.ascontiguousarray(f(inp["ssm_norm"])[0].reshape(16, 128).T)
    sh["ln1g"] = rep(inp["ln1_g"][0], 1024)
    sh["ln1b"] = rep(inp["ln1_b"][0], 1024)
    sh["ln2g"] = rep(inp["ln2_g"][0], 1024)
    sh["ln2b"] = rep(inp["ln2_b"][0], 1024)
    sh.update(host_consts())
    return sh


def core_inputs(inp, sh, b):
    m = dict(sh)
    m["x"] = np.ascontiguousarray(np.asarray(inp["x"][b], dtype=np.float32))
    m["cT"] = np.ascontiguousarray(np.asarray(inp["c"][b], dtype=np.float32).reshape(8, 128).T)
    return m


_CACHE = {}


def kernel(**inputs):
    if "nc" not in _CACHE:
        _CACHE["nc"] = build(32)[0]
    nc = _CACHE["nc"]
    sh = host_shared(inputs)
    in_maps = [core_inputs(inputs, sh, b) for b in range(NCORE)]
    res = run_bass_kernel_spmd(nc, in_maps, core_ids=list(range(NCORE)))
    out = np.stack([np.asarray(r["out"]).reshape(SEQ, D) for r in res.results], axis=0)
    return out.astype(np.float32)
```

```python
from contextlib import ExitStack
import numpy as np
import concourse.bass as bass
import concourse.mybir as mybir
from concourse.bass_utils import run_bass_kernel_spmd

F32 = mybir.dt.float32
BF16 = mybir.dt.bfloat16
ALU = mybir.AluOpType
AF = mybir.ActivationFunctionType
ENGS = ("pe", "act", "dve", "pool", "sp")

D = 1024
SEQ = 4096
NCORE = 8
ALPHA = 2.0 ** 0.25
LN_EPS = 1e-5
RMS_EPS = 1e-6
D_FF = 2816


class T:
    def __init__(self, name, handle):
        self.name = name
        self.h = handle

    def __getitem__(self, k):
        return self.h[k]


class Op:
    __slots__ = ("eng", "fn", "deps", "is_dma", "dsem", "dcount", "cost", "nbytes", "pos", "count", "signal",
                 "start", "finish", "prio", "users", "tcls")

    def __init__(self, eng, fn, is_dma=False, dsem=None, cost=100.0, nbytes=0):
        self.eng = eng
        self.fn = fn
        self.deps = set()
        self.is_dma = is_dma
        self.dsem = dsem
        self.dcount = None
        self.cost = cost
        self.nbytes = nbytes
        self.pos = None
        self.count = None
        self.signal = False
        self.start = 0.0
        self.finish = 0.0
        self.prio = 0.0
        self.users = []
        self.tcls = None


class Prog:
    def __init__(self, nc):
        self.nc = nc
        self.ops = []
        self.track = {}
        self.stack = ExitStack()
        self.dma_sem_counts = {}

    def sb(self, name, shape, dtype=F32):
        return T("s_" + name, self.stack.enter_context(self.nc.sbuf_tensor("s_" + name, list(shape), dtype)))

    def ps(self, name, shape, dtype=F32):
        return T("p_" + name, self.stack.enter_context(self.nc.psum_tensor("p_" + name, list(shape), dtype)))

    def _conf(self, tn, sub):
        d = self.track.setdefault(tn, {})
        if sub is None:
            return list(d.values())
        out = []
        if sub in d:
            out.append(d[sub])
        if None in d:
            out.append(d[None])
        return out

    @staticmethod
    def _norm(lst):
        out = []
        for it in lst:
            if isinstance(it, tuple):
                t, sub = it
            else:
                t, sub = it, None
            out.append((t.name if isinstance(t, T) else t, sub))
        return out

    def add(self, eng, fn, reads=(), writes=(), is_dma=False, dsem=None, cost=100.0, nbytes=0, tcls=None):
        reads = self._norm(reads)
        writes = self._norm(writes)
        op = Op(eng, fn, is_dma, dsem, cost, nbytes)
        op.tcls = tcls
        idx = len(self.ops)
        for (tn, sub) in reads:
            for rec in self._conf(tn, sub):
                if rec[0] is not None:
                    op.deps.add(rec[0])
                if tn.startswith("p_"):
                    for r_ in rec[1]:
                        if self.ops[r_].eng != eng:
                            op.deps.add(r_)
        for (tn, sub) in writes:
            for rec in self._conf(tn, sub):
                if rec[0] is not None:
                    op.deps.add(rec[0])
                op.deps.update(rec[1])
        for (tn, sub) in reads:
            d = self.track.setdefault(tn, {})
            d.setdefault(sub, [None, []])[1].append(idx)
        for (tn, sub) in writes:
            d = self.track.setdefault(tn, {})
            if sub is None:
                d.clear()
            d[sub] = [idx, []]
        op.deps.discard(idx)
        if is_dma:
            c = self.dma_sem_counts.get(dsem, 0) + 1
            self.dma_sem_counts[dsem] = c
            op.dcount = 16 * c
        self.ops.append(op)
        return idx

    def schedule(self, reorder=True):
        import heapq
        ops = self.ops
        n = len(ops)
        SEM_LAT = 250.0
        if not reorder:
            per = {e: [] for e in ENGS}
            for op in ops:
                per[op.eng].append(op)
            return per
        for i, op in enumerate(ops):
            for d_ in op.deps:
                ops[d_].users.append(i)
        for i in range(n - 1, -1, -1):
            op = ops[i]
            best = 0.0
            for u in op.users:
                if ops[u].prio > best:
                    best = ops[u].prio
            op.prio = best + op.cost + (2000.0 if op.is_dma else 0.0)
        indeg = [len(op.deps) for op in ops]
        ready_t = [0.0] * n
        ready = {e: [] for e in ENGS}
        for i, op in enumerate(ops):
            if indeg[i] == 0:
                heapq.heappush(ready[op.eng], (-op.prio, i))
        free_at = {e: 0.0 for e in ENGS}
        cur_tab = [None]
        TAB_NS = 1283.0
        dma_bw_free = 0.0
        per = {e: [] for e in ENGS}
        done = 0
        WIN = 1
        while done < n:
            best = None
            for e in ENGS:
                h = ready[e]
                if not h:
                    continue
                cands = heapq.nsmallest(WIN_E.get(e, WIN), h)
                for (np_, i) in cands:
                    st = max(free_at[e], ready_t[i])
                    if e == "act" and ops[i].tcls is not None and ops[i].tcls != cur_tab[0]:
                        st += TAB_NS
                    key = (st, np_)
                    if best is None or key < best[0]:
                        best = (key, e, i, np_)
            assert best is not None, "scheduler deadlock"
            (st, _), e, i, np_ = best
            ready[e].remove((np_, i))
            heapq.heapify(ready[e])
            op = ops[i]
            op.start = st
            if op.is_dma:
                free_at[e] = st + 60.0
                t0 = max(st + 1500.0, dma_bw_free)
                op.finish = t0 + op.nbytes / 250.0
                dma_bw_free = op.finish
            else:
                if e == "act" and op.tcls is not None:
                    cur_tab[0] = op.tcls
                op.finish = st + op.cost
                free_at[e] = op.finish
            per[e].append(op)
            done += 1
            for u in op.users:
                lat = 0.0 if (ops[u].eng == e and not op.is_dma and e == "pe") else SEM_LAT
                if op.finish + lat > ready_t[u]:
                    ready_t[u] = op.finish + lat
                indeg[u] -= 1
                if indeg[u] == 0:
                    heapq.heappush(ready[ops[u].eng], (-ops[u].prio, u))
        self.est_ns = max(op.finish for op in ops)
        return per

    def emit(self, reorder=True):
        nc = self.nc
        st = self.stack
        ops = self.ops
        per = self.schedule(reorder)
        for e in ENGS:
            for p_, op in enumerate(per[e]):
                op.pos = p_
        dcnt = {}
        dmas = [(op.start if reorder else 0.0, i) for i, op in enumerate(ops) if op.is_dma]
        dmas.sort()
        for _, i in dmas:
            op = ops[i]
            dcnt[op.dsem] = dcnt.get(op.dsem, 0) + 1
            op.dcount = 16 * dcnt[op.dsem]
        def needs_sem(cons, prod):
            if prod.is_dma:
                return True
            if cons.eng == "pe" and prod.eng == "pe" and not cons.is_dma:
                return False
            return True
        need = []
        for op in ops:
            nd = {}
            for x in op.deps:
                p = ops[x]
                if not needs_sem(op, p):
                    continue
                key = ("d", p.dsem) if p.is_dma else ("e", p.eng)
                cur = nd.get(key)
                if p.is_dma:
                    if cur is None or p.dcount > cur.dcount:
                        nd[key] = p
                else:
                    if cur is None or p.pos > cur.pos:
                        nd[key] = p
            need.append(nd)
            for p in nd.values():
                p.signal = True
        esem = {e: st.enter_context(nc.semaphore("sem_" + e)) for e in ENGS}
        dsems = {k: st.enter_context(nc.semaphore("dsem_" + str(k))) for k in self.dma_sem_counts}
        for e in ENGS:
            c = 0
            for op in per[e]:
                if op.signal and not op.is_dma:
                    c += 1
                    op.count = c
        opidx = {id(op): i for i, op in enumerate(ops)}

        def run(engname, eng):
            waited = {}
            for op in per[engname]:
                nd = need[opidx[id(op)]]
                for key, p in nd.items():
                    val = p.dcount if p.is_dma else p.count
                    if waited.get(key, 0) >= val:
                        continue
                    waited[key] = val
                    eng.wait_ge(dsems[key[1]] if key[0] == "d" else esem[key[1]], val)
                inst = op.fn(eng)
                if op.is_dma:
                    inst.then_inc(dsems[op.dsem], 16)
                elif op.signal:
                    inst.then_inc(esem[op.eng], 1)
            if engname in ("sp", "pool", "act"):
                for k, c in dcnt.items():
                    if str(k).startswith("out") and engname == "sp":
                        eng.wait_ge(dsems[k], 16 * c)

        with nc.Block() as block:
            @block.tensor
            def _(e):
                run("pe", e)

            @block.scalar
            def _(e):
                run("act", e)

            @block.vector
            def _(e):
                run("dve", e)

            @block.gpsimd
            def _(e):
                run("pool", e)

            @block.sync
            def _(e):
                run("sp", e)
        st.close()


FAM = {
    "wada": (1024, 6144, 8, 512),
    "wq": (1024, 1024, 8, 512), "wf": (1024, 1024, 8, 512), "wi": (1024, 1024, 8, 512),
    "wg": (1024, 1024, 8, 512), "wz": (1024, 2048, 8, 512), "wxbc": (1024, 3072, 8, 512),
    "wdt": (1024, 32, 8, 32), "wga": (1024, 1024, 8, 512), "wgb": (1024, 1024, 8, 512),
    "wba": (1024, 1024, 8, 512), "wbb": (2048, 1024, 8, 512), "wo": (1024, 1024, 8, 512),
    "wgate": (1024, D_FF, 8, 256), "wup": (1024, D_FF, 8, 256), "wdown": (D_FF, 1024, 11, 256),
}


def fam_blocks(name):
    K, C, kb, cw = FAM[name]
    return C // cw, (K // 128) // kb


def blockify(W, name):
    K, C, kb, cw = FAM[name]
    ncb, nkg = fam_blocks(name)
    a = np.ascontiguousarray(W, dtype=np.float32).reshape(nkg, kb, 128, ncb, cw)
    a = a.transpose(3, 0, 2, 1, 4)
    return np.ascontiguousarray(a.reshape(ncb * nkg, 128, kb * cw))


REORDER = True
WIN_E = {"pe": 8, "act": 8, "dve": 2, "pool": 1, "sp": 1}


def build(NT=32, dbg=None):
    nc = bass.Bass("TRN2", target_bir_lowering=False)
    P = Prog(nc)
    dram = {}

    def din(name, shape):
        dram[name] = nc.dram_tensor(name, list(shape), F32, kind="ExternalInput").ap()
        return dram[name]

    x_d = din("x", [SEQ, D])
    out_d = nc.dram_tensor("out", [NT * 128, D], F32, kind="ExternalOutput").ap()
    for fn_ in FAM:
        K, C, kb, cw = FAM[fn_]
        ncb, nkg = fam_blocks(fn_)
        din(fn_, [ncb * nkg, 128, kb * cw])
    cT_d = din("cT", [128, 8])
    bada_d = din("bada", [1, 6144])
    lbT_d = din("lbT", [128, 16])
    gnorm_d = din("gnorm", [128, 1])
    convw_d = din("convw", [128, 24 * 4])
    convb_d = din("convb", [128, 24])
    dtb_d = din("dtb", [128, 32])
    alog_d = din("alog", [128, 32])
    dsk_d = din("dsk", [128, 32])
    snorm_d = din("snorm", [128, 16])
    ln_d = {k: din(k, [128, 1024]) for k in ("ln1g", "ln1b", "ln2g", "ln2b")}
    ident_d = din("ident", [128, 128])
    tri_d = din("tri", [128, 128])
    hmask_d = din("hmask", [128, 128])
    negm_d = din("negm", [128, 512])
    smask_d = din("smask", [128, 512])
    dbg_out = {}

    NSLOT = 4
    slots = [P.sb("wslot%d" % i, [128, 4096], BF16) for i in range(NSLOT)]
    slot_ctr = [0]
    NSUB = 2
    TT = NSUB * 128
    xt = [P.sb("xt%d" % i, [128, 1024]) for i in range(2 * NSUB)]
    class Cx:
        pass
    cxA, cxB, cxP = Cx(), Cx(), Cx()
    for cx_, sfx in ((cxA, ""), (cxB, "_b")):
        cx_.xn = P.sb("xn" + sfx, [128, 1024], BF16)
        cx_.uT = P.sb("uT" + sfx, [128, 8, TT], BF16)
        cx_.st6 = P.sb("st6" + sfx, [128, 12])
        cx_.mv = P.sb("mv" + sfx, [128, 2])
        cx_.rstd = P.sb("rstd" + sfx, [128, 1])
        cx_.nmr = P.sb("nmr" + sfx, [128, 1])
        cx_.bctr = 0
    uT = cxA.uT
    CUR = [cxP]
    epsln = P.sb("epsln", [128, 1])
    epsrms = P.sb("epsrms", [128, 1])
    identf = P.sb("identf", [128, 128])
    identb = P.sb("identb", [128, 128], BF16)
    tri = P.sb("tri", [128, 128])
    onesf = P.sb("onesf", [128, 128])
    hmask = P.sb("hmask", [128, 128])
    negmb = P.sb("negmb", [128, 512], BF16)
    smask = P.sb("smask", [128, 512])
    lbraw = P.sb("lbraw", [128, 16])
    lb = P.sb("lb", [128, 8])
    oml = P.sb("oml", [128, 8])
    cTt = P.sb("cTt", [128, 8])
    condb = P.sb("condb", [128, 8], BF16)
    bada = P.sb("bada", [1, 512])
    modrow = P.sb("modrow", [1, 512])
    modc = P.sb("modc", [128, 32])
    lnt = {k: P.sb(k + "_t", [128, 1024]) for k in ln_d}
    gnorm = P.sb("gnorm_t", [128, 1])
    convw = P.sb("convw_t", [128, 24, 4])
    convb = P.sb("convb_t", [128, 24])
    dtb = P.sb("dtb_t", [128, 32])
    negA = P.sb("negA", [128, 32])
    dsk = P.sb("dsk_t", [128, 32])
    snorm = P.sb("snorm_t", [128, 16])
    Sh = P.sb("Sh", [128, 8, 128])
    Shbf = P.sb("Shbf", [128, 8, 128], BF16)
    Sm = P.sb("Sm", [128, 4, 512])
    Smbf = P.sb("Smbf", [128, 4, 512], BF16)
    hist = P.sb("hist", [128, 24, 3])
    sig_h = P.sb("sig_h", [128, 4, TT])
    sq_h = P.sb("sq_h", [128, 4, TT], BF16)
    sg_h = P.sb("sg_h", [128, 4, TT], BF16)
    v_h = P.sb("v_h", [128, NSUB, 512], BF16)
    sga = P.sb("sga", [128, 4, TT], BF16)
    sgb = P.sb("sgb", [128, 4, TT], BF16)
    sz = P.sb("sz", [128, NSUB, 512], BF16)
    xs_tok = P.sb("xs_tok", [128, NSUB, 512], BF16)
    BT = P.sb("BT", [128, 4, TT], BF16)
    CT = P.sb("CT", [128, 4, TT], BF16)
    B_tok = P.sb("B_tok", [128, NSUB, 4, 128], BF16)
    raw = P.sb("raw", [128, 4, TT + 3])
    cacc = P.sb("cacc", [128, 4, TT])
    xc = P.sb("xc", [128, 4, TT], BF16)
    dtp = P.sb("dtp", [128, 32])
    a_tok = P.sb("a_tok", [128, 32])
    delta = P.sb("delta", [128, NSUB, 32])
    acum_tok = P.sb("acum_tok", [128, NSUB, 32])
    nacum_tok = P.sb("nacum_tok", [128, NSUB, 32])
    eacum = P.sb("eacum", [128, NSUB, 32])
    w_tok = P.sb("w_tok", [128, NSUB, 32])
    ealast = P.sb("ealast", [128, NSUB, 32])
    dw = P.sb("dw", [128, NSUB, 32])
    acumT = P.sb("acumT", [32, NSUB, 128])
    rhsx = [P.sb("rhsx0", [32, 4, 128])]
    xdt = P.sb("xdt", [128, 512], BF16)
    xdtw = P.sb("xdtw", [128, 512], BF16)
    decT = [P.sb("decT%d" % i, [128, 4, 128], BF16) for i in range(2)]
    MT = P.sb("MT", [128, 8, 128], BF16)
    t1 = P.sb("t1", [128, 512])
    t2 = P.sb("t2", [128, 512])
    xsD = P.sb("xsD", [128, 512])
    ssq = P.sb("ssq", [128, 1])
    rs_m = P.sb("rs_m", [128, 1])
    yb_tok = P.sb("yb_tok", [128, 512], BF16)
    ybT = P.sb("ybT", [128, 16, TT], BF16)
    ff = P.sb("ff", [128, 4, 128])
    kk = P.sb("kk", [128, 4, 128])
    bc = P.sb("bc", [128, 4, 2, 64])
    dd = P.sb("dd", [128, 4, 2, 64])
    E1 = P.sb("E1", [128, 4, 128], BF16)
    elast = P.sb("elast", [128, 4, 2])
    qh = P.sb("qh", [128, 4, 128], BF16)
    kh = P.sb("kh", [128, 4, 128], BF16)
    kh_tok = P.sb("kh_tok", [128, 4, 128], BF16)
    scm = P.sb("scm", [128, 4, 128], BF16)
    o_sb = ff
    osq = kk
    haT = P.sb("haT", [128, 8, TT], BF16)
    m1 = P.sb("m1", [128, 4, TT])
    m2 = P.sb("m2", [128, 4, TT])
    g1b = T(m1.name, m1[:].rearrange("p a t -> p (a t)"))
    g2b = T(m2.name, m2[:].rearrange("p a t -> p (a t)"))
    mT = P.sb("mT", [128, 8, TT], BF16)
    ftmp = P.sb("ftmp", [128, 2, TT], BF16)
    hmT = P.sb("hmT", [128, 22, TT], BF16)

    banks = [P.ps("bank%d" % i, [128, 512]) for i in range(8)]
    cxP.banks = banks
    cxP.bctr = 0
    cxA.banks = banks[0:6]
    cxB.banks = banks[6:8]

    def nb():
        cx = CUR[0]
        b = cx.banks[cx.bctr % len(cx.banks)]
        cx.bctr += 1
        return b

    def fsz(ap):
        n = 1
        for d_ in ap.shape[1:]:
            n *= int(d_)
        return n

    def ecost(eng, ap):
        f = fsz(ap)
        if eng == "act":
            return 200.0 + 0.6 * f
        if eng == "pool":
            return 300.0 + 7.0 * f
        return 100.0 + 1.04 * f

    def dma(eng, out_ap, in_ap, reads, writes, dsem):
        nb_ = fsz(out_ap) * int(out_ap.shape[0]) * (2 if out_ap.dtype == BF16 else 4)
        P.add(eng, lambda e: e.dma_start(out=out_ap, in_=in_ap), reads=reads, writes=writes, is_dma=True, dsem=dsem, nbytes=nb_)

    def mm(out_ap, lhsT, rhs, start, stop, reads, writes, sgc=False):
        n_ = fsz(rhs)
        if lhsT.dtype == F32:
            c_ = 160.0 + 1.3 * n_
        elif n_ <= 64:
            c_ = 50.0
        elif n_ <= 128:
            c_ = 101.0
        elif n_ <= 256:
            c_ = 131.0
        else:
            c_ = 351.0
        if sgc:
            P.add("pe", lambda e: e.matmul(out_ap, lhsT=lhsT, rhs=rhs, start=start, stop=stop, skip_group_check=True), reads=reads, writes=writes, cost=c_)
        else:
            P.add("pe", lambda e: e.matmul(out_ap, lhsT=lhsT, rhs=rhs, start=start, stop=stop), reads=reads, writes=writes, cost=c_)

    def act(out_ap, in_ap, func, reads, writes, bias=None, scale=None, accum_out=None):
        kw = {}
        if bias is not None:
            kw["bias"] = bias
        if scale is not None:
            kw["scale"] = scale
        if accum_out is not None:
            kw["accum_out"] = accum_out
        tc_ = {AF.Sigmoid: "sig", AF.Silu: "silu", AF.Tanh: "silu", AF.Exp: "el", AF.Ln: "el"}.get(func)
        P.add("act", lambda e: e.activation(out=out_ap, in_=in_ap, func=func, **kw), reads=reads, writes=writes, cost=ecost("act", out_ap), tcls=tc_)

    def tt(out_ap, in0, in1, op, reads, writes, eng="dve"):
        P.add(eng, lambda e: e.tensor_tensor(out=out_ap, in0=in0, in1=in1, op=op), reads=reads, writes=writes, cost=ecost(eng, out_ap))

    def ts(out_ap, in0, s1, s2, op0, op1, reads, writes, eng="dve"):
        c_ = ecost(eng, out_ap)
        if op1 is None:
            P.add(eng, lambda e: e.tensor_scalar(out=out_ap, in0=in0, scalar1=s1, scalar2=None, op0=op0), reads=reads, writes=writes, cost=c_)
        else:
            P.add(eng, lambda e: e.tensor_scalar(out=out_ap, in0=in0, scalar1=s1, scalar2=s2, op0=op0, op1=op1), reads=reads, writes=writes, cost=c_)

    def stt(out_ap, in0, scalar, in1, op0, op1, reads, writes, eng="dve"):
        P.add(eng, lambda e: e.scalar_tensor_tensor(out=out_ap, in0=in0, scalar=scalar, in1=in1, op0=op0, op1=op1), reads=reads, writes=writes,
              cost=ecost(eng, out_ap))

    def cp(out_ap, in_ap, reads, writes, eng="dve"):
        c_ = ecost(eng, out_ap)
        if eng == "act":
            P.add("act", lambda e: e.copy(out=out_ap, in_=in_ap), reads=reads, writes=writes, cost=c_)
        else:
            P.add(eng, lambda e: e.tensor_copy(out=out_ap, in_=in_ap), reads=reads, writes=writes, cost=c_)

    def memset(t, val, eng="dve"):
        P.add(eng, lambda e: e.memset(t[:], val), writes=[t])

    scratch = {}
    for fam_ in FAM:
        if fam_ == "wada":
            continue
        K_, C_, kb_, cw_ = FAM[fam_]
        ncb_, nkg_ = fam_blocks(fam_)
        scratch[fam_] = nc.dram_tensor("ws_" + fam_, [ncb_ * nkg_, 128, kb_ * cw_], BF16, kind="Internal").ap()

    converted = set()

    def wload(fam, blk, from_f32=False):
        K, C, kb, cw = FAM[fam]
        i = slot_ctr[0] % NSLOT
        slot_ctr[0] += 1
        s = slots[i]
        if from_f32:
            dma("pool", s[:, 0:kb * cw], dram[fam][blk], reads=[], writes=[s], dsem="wc%d" % i)
        elif (fam, blk) not in converted:
            converted.add((fam, blk))
            dma("pool", s[:, 0:kb * cw], dram[fam][blk], reads=[], writes=[s], dsem="wc%d" % i)
            ncb, nkg = fam_blocks(fam)
            cbi, kgi = blk // nkg, blk % nkg
            s3 = s[:, 0:kb * cw].rearrange("p (k c) -> p k c", k=kb)
            if fam == "wo":
                tt(s3, s3, bcast(g1b[:, cbi * cw:(cbi + 1) * cw].unsqueeze(1), [128, kb, cw]), ALU.mult, [s, g1b], [s])
            elif fam == "wdown":
                tt(s3, s3, bcast(g2b[:, cbi * cw:(cbi + 1) * cw].unsqueeze(1), [128, kb, cw]), ALU.mult, [s, g2b], [s])
            elif fam == "wbb":
                tt(s3, s3, bcast(snorm[:, kgi * kb:(kgi + 1) * kb].unsqueeze(2), [128, kb, cw]), ALU.mult, [s, snorm], [s])
            dma("sp", scratch[fam][blk], s[:, 0:kb * cw], reads=[s], writes=[("ws_" + fam, blk)], dsem="wst%d" % i)
        else:
            dma("sp", s[:, 0:kb * cw], scratch[fam][blk], reads=[("ws_" + fam, blk)], writes=[s], dsem="w%d" % i)
        return s

    def wv(s, fam, k, c0, n):
        K, C, kb, cw = FAM[fam]
        return s[:, k * cw + c0: k * cw + c0 + n]

    def bcast(ap, shape):
        return ap.to_broadcast(list(shape))

    cq = ["sp"]

    def cload(t, d, name):
        dma("sp", t[:], d, reads=[], writes=[t], dsem="c_" + name)

    cload(identf, ident_d, "ident")
    cload(tri, tri_d, "tri")
    cload(hmask, hmask_d, "hmask")
    dma("pool", negmb[:], negm_d, reads=[], writes=[negmb], dsem="c_negm")
    cload(smask, smask_d, "smask")
    cload(lbraw, lbT_d, "lb")
    cload(cTt, cT_d, "cT")
    cload(gnorm, gnorm_d, "gnorm")
    dma("sp", convw[:].rearrange("p a b -> p (a b)"), convw_d, reads=[], writes=[convw], dsem="c_convw")
    cload(convb, convb_d, "convb")
    cload(dtb, dtb_d, "dtb")
    cload(negA, alog_d, "alog")
    cload(dsk, dsk_d, "dsk")
    cload(snorm, snorm_d, "snorm")
    for k_ in ln_d:
        cload(lnt[k_], ln_d[k_], k_)
    memset(onesf, 1.0)
    memset(epsln, LN_EPS)
    memset(epsrms, RMS_EPS)
    memset(Sh, 0.0)
    memset(Sm, 0.0)
    memset(Smbf, 0.0)
    memset(hist, 0.0)
    cp(identb[:], identf[:], [identf], [identb])
    act(negA[:], negA[:], AF.Exp, [negA], [negA])
    ts(negA[:], negA[:], -1.0, None, ALU.mult, None, [negA], [negA])
    tt(lb[:], lbraw[:, 0:8], lbraw[:, 8:16], ALU.subtract, [lbraw], [lb])
    act(lb[:], lb[:], AF.Sigmoid, [lb], [lb])
    ts(oml[:], lb[:], -0.5, 0.5, ALU.mult, ALU.add, [lb], [oml])
    tt(lb[:], lb[:], oml[:], ALU.add, [lb, oml], [lb])
    act(condb[:], cTt[:], AF.Silu, [cTt], [condb])
    for cb in range(12):
        s = wload("wada", cb, from_f32=True)
        b = nb()
        for k in range(8):
            mm(b[0:1, 0:512], condb[:, k:k + 1], wv(s, "wada", k, 0, 512), k == 0, k == 7, [condb, s], [b])
        dma("sp", bada[:], bada_d[0:1, cb * 512:(cb + 1) * 512], reads=[], writes=[bada], dsem="c_bada")
        tt(modrow[0:1, :], b[0:1, 0:512], bada[0:1, :], ALU.add, [b, bada], [modrow])
        sec, half = cb // 2, cb % 2
        if sec in (2, 5):
            gt = g1b if sec == 2 else g2b
            b2 = nb()
            mm(b2[:, 0:512], onesf[0:1, 0:128], modrow[0:1, :], True, True, [onesf, modrow], [b2])
            ts(gt[:, half * 512:(half + 1) * 512], b2[:, 0:512], (0.5 if sec == 2 else 1.0), None, ALU.mult, None, [b2], [(gt, half)])
        else:
            vi = {0: 0, 1: 1, 3: 2, 4: 3}[sec]
            b2 = nb()
            for j in range(4):
                mm(b2[:, j:j + 1], modrow[0:1, j * 128:(j + 1) * 128], onesf[0:1, 0:1], True, True, [modrow, onesf], [b2])
            c0 = vi * 8 + half * 4
            cp(modc[:, c0:c0 + 4], b2[:, 0:4], [b2], [(modc, (vi, half))])
            if vi in (1, 3) and half == 1:
                ts(modc[:, vi * 8:(vi + 1) * 8], modc[:, vi * 8:(vi + 1) * 8], 1.0, None, ALU.add, None, [(modc, (vi, 0)), (modc, (vi, 1))], [(modc, (vi, 0)), (modc, (vi, 1))])

    for blk_ in range(2):
        wload("wo", blk_)
    for blk_ in range(8):
        wload("wdown", blk_)

    def layernorm_stats(src):
        st6, mv, rstd = CUR[0].st6, CUR[0].mv, CUR[0].rstd
        P.add("dve", lambda e: e.bn_stats(out=st6[:, 0:6], in_=src[:, 0:512]), reads=[src], writes=[(st6, 0)], cost=650.0)
        P.add("dve", lambda e: e.bn_stats(out=st6[:, 6:12], in_=src[:, 512:1024]), reads=[src], writes=[(st6, 1)], cost=650.0)
        P.add("dve", lambda e: e.bn_aggr(out=mv[:], in_=st6[:]), reads=[st6], writes=[mv])
        act(rstd[:], mv[:, 1:2], AF.Ln, [mv, epsln], [rstd], bias=epsln[:, 0:1], scale=1.0)
        act(rstd[:], rstd[:], AF.Exp, [rstd], [rstd], scale=-0.5)

    def make_uT(src, sc_off, sh_off, sub):
        cx = CUR[0]
        mv, rstd, nmr, xn, uT = cx.mv, cx.rstd, cx.nmr, cx.xn, cx.uT
        layernorm_stats(src)
        stt(nmr[:], mv[:, 0:1], -1.0, rstd[:], ALU.mult, ALU.mult, [mv, rstd], [nmr])
        act(xn[:], src[:], AF.Identity, [src, rstd, nmr], [xn], bias=nmr[:, 0:1], scale=rstd[:, 0:1])
        for half in range(2):
            b = nb()
            for c in range(4):
                k = half * 4 + c
                mm(b[:, c * 128:(c + 1) * 128], xn[:, k * 128:(k + 1) * 128], identb[:], True, True, [xn, identb], [b])
            for c in range(4):
                k = half * 4 + c
                act(uT[:, k, sub * 128:(sub + 1) * 128], b[:, c * 128:(c + 1) * 128], AF.Identity, [b] + [(modc, (v_, h_)) for v_ in (sh_off // 8, sc_off // 8) for h_ in (0, 1)], [(uT, (k, sub))],
                    bias=modc[:, sh_off + k: sh_off + k + 1], scale=modc[:, sc_off + k: sc_off + k + 1])

    def proj_fm(fam, blk, ntiles=4):
        s = wload(fam, blk)
        uT = CUR[0].uT
        bks = [nb() for _ in range((ntiles + 1) // 2)]
        for c in range(ntiles):
            b = bks[c // 2]
            for k in range(8):
                mm(b[:, (c % 2) * TT:(c % 2 + 1) * TT], wv(s, fam, k, c * 128, 128), uT[:, k, :], k == 0, k == 7, [s, uT], [b])
        return bks

    def b3(bank):
        return bank[:].rearrange("p (a t) -> p a t", a=2)

    def proj_tm_sub(s, fam, sub, ncols):
        uT = CUR[0].uT
        b = nb()
        for k in range(8):
            mm(b[:, 0:ncols], uT[:, k, sub * 128:(sub + 1) * 128], wv(s, fam, k, 0, ncols), k == 0, k == 7, [s, uT], [b])
        return b

    def dbg_tap(name, tile_ap, shape, rd):
        if dbg is not None and name in dbg:
            o = nc.dram_tensor("dbg_" + name, list(shape), tile_ap.dtype, kind="ExternalOutput").ap()
            dbg_out[name] = o
            dma("sp", o, tile_ap, reads=rd, writes=[], dsem="out_dbg_" + name)

    NST = NT // NSUB
    assert NT % NSUB == 0

    def xbuf(st, sub):
        return xt[(st % 2) * NSUB + sub]

    def load_x(st):
        for sub in range(NSUB):
            Xn = xbuf(st, sub)
            ti = st * NSUB + sub
            dma("pool", Xn[:], x_d[ti * 128:(ti + 1) * 128, :], reads=[], writes=[Xn], dsem="x%d" % ((st % 2) * NSUB + sub))

    def conv_block(blk, XC):
        bks = proj_fm("wxbc", blk)
        for hb in range(2):
            cp(raw[:, hb * 2:(hb + 1) * 2, 3:3 + TT], b3(bks[hb]), [bks[hb]], [(raw, ("m", hb))], eng="act")
        cp(raw[:, :, 0:3], hist[:, blk * 4:(blk + 1) * 4, :], [(hist, blk)], [(raw, "h")], eng="pool")
        cp(hist[:, blk * 4:(blk + 1) * 4, :], raw[:, :, TT:TT + 3], [(raw, ("m", 0)), (raw, ("m", 1))], [(hist, blk)], eng="pool")
        for kk_ in range(4):
            for c in range(4):
                j = blk * 4 + c
                if kk_ == 0:
                    ts(cacc[:, c, :], raw[:, c, 0:TT], convw[:, j, 0:1], None, ALU.mult, None, [raw, convw], [(cacc, c)])
                else:
                    stt(cacc[:, c, :], raw[:, c, kk_:kk_ + TT], convw[:, j, kk_:kk_ + 1], cacc[:, c, :], ALU.mult, ALU.add,
                        [raw, convw, (cacc, c)], [(cacc, c)])
        for c in range(4):
            j = blk * 4 + c
            act(XC[:, c, :], cacc[:, c, :], AF.Silu, [(cacc, c), convb], [(XC, c)], bias=convb[:, j:j + 1], scale=1.0)

    def dt_chain(s, sub):
        b = proj_tm_sub(s, "wdt", sub, 32)
        tt(dtp[:], b[:, 0:32], dtb[:], ALU.add, [b, dtb], [dtp])
        act(delta[:, sub, :], dtp[:], AF.Exp, [dtp], [(delta, sub)])
        act(delta[:, sub, :], delta[:, sub, :], AF.Ln, [(delta, sub)], [(delta, sub)], bias=1.0, scale=1.0)
        tt(a_tok[:], delta[:, sub, :], negA[:], ALU.mult, [(delta, sub), negA], [a_tok])
        p1 = nb()
        mm(p1[:, 0:32], tri[:], a_tok[:], True, True, [tri, a_tok], [p1])
        mm(p1[:, 32:64], onesf[:], a_tok[:], True, True, [onesf, a_tok], [p1])
        mm(p1[0:32, 64:192], a_tok[:], tri[:], True, True, [tri, a_tok], [p1])
        cp(acum_tok[:, sub, :], p1[:, 0:32], [p1], [(acum_tok, sub)])
        ts(nacum_tok[:, sub, :], acum_tok[:, sub, :], -1.0, None, ALU.mult, None, [(acum_tok, sub)], [(nacum_tok, sub)])
        act(eacum[:, sub, :], p1[:, 0:32], AF.Exp, [p1], [(eacum, sub)])
        tt(w_tok[:, sub, :], p1[:, 32:64], acum_tok[:, sub, :], ALU.subtract, [p1, (acum_tok, sub)], [(w_tok, sub)])
        act(w_tok[:, sub, :], w_tok[:, sub, :], AF.Exp, [(w_tok, sub)], [(w_tok, sub)])
        act(ealast[:, sub, :], p1[:, 32:64], AF.Exp, [p1], [(ealast, sub)])
        tt(dw[:, sub, :], delta[:, sub, :], w_tok[:, sub, :], ALU.mult, [(delta, sub), (w_tok, sub)], [(dw, sub)])
        cp(acumT[:, sub, :], p1[0:32, 64:192], [p1], [(acumT, sub)])

    def mamba_core(sub, g):
        hs = slice(g * 8, (g + 1) * 8)
        tc_ = slice(sub * 128, (sub + 1) * 128)
        dl_b = bcast(delta[:, sub, hs].unsqueeze(2), [128, 8, 64])
        dw_b = bcast(dw[:, sub, hs].unsqueeze(2), [128, 8, 64])
        xs3 = xs_tok[:, sub, :].rearrange("p (h q) -> p h q", h=8)
        tt(xdt[:].rearrange("p (h q) -> p h q", h=8), xs3, dl_b, ALU.mult, [(xs_tok, sub), (delta, sub)], [xdt])
        tt(xdtw[:].rearrange("p (h q) -> p h q", h=8), xs3, dw_b, ALU.mult, [(xs_tok, sub), (dw, sub)], [xdtw])
        bcb = nb()
        mm(bcb[:, 0:128], BT[:, g, tc_], CT[:, g, tc_], True, True, [BT, CT], [bcb])
        R = rhsx[0]
        for hh in range(2):
            h0 = g * 8 + hh * 4
            tt(R[:], bcast(acumT[:, sub, :].unsqueeze(1), [32, 4, 128]), bcast(identf[0:32, h0:h0 + 4].unsqueeze(2), [32, 4, 128]),
               ALU.mult, [(acumT, sub), identf], [R])
            bd = nb()
            DT_ = decT[hh]
            mm(bd[:, 0:512], onesf[0:32, :], R[:].rearrange("p a t -> p (a t)"), True, False, [onesf, R], [bd])
            mm(bd[:, 0:512], identb[:], negmb[:], False, True, [identb, negmb], [bd])
            for hl in range(4):
                h = g * 8 + hh * 4 + hl
                act(DT_[:, hl, :], bd[:, hl * 128:(hl + 1) * 128], AF.Exp, [bd, (nacum_tok, sub)], [(DT_, hl)],
                    bias=nacum_tok[:, sub, h:h + 1], scale=1.0)
            tt(MT[:, hh * 4:(hh + 1) * 4, :], DT_[:], bcast(bcb[:, 0:128].unsqueeze(1), [128, 4, 128]), ALU.mult,
               [DT_, bcb], [(MT, hh)])
        yield
        by = nb()
        for h in range(8):
            mm(by[:, h * 64:(h + 1) * 64], MT[:, h, :], xdt[:, h * 64:(h + 1) * 64], True, True, [MT, xdt], [by])
        bi = nb()
        mm(bi[:, 0:512], CT[:, g, tc_], Smbf[:, g, :], True, True, [CT, (Smbf, g)], [bi])
        ea_b = bcast(eacum[:, sub, hs].unsqueeze(2), [128, 8, 64])
        tt(t1[:].rearrange("p (h q) -> p h q", h=8), bi[:].rearrange("p (h q) -> p h q", h=8), ea_b, ALU.mult, [bi, (eacum, sub)], [t1])
        tt(t2[:], t1[:], by[:, 0:512], ALU.add, [t1, by], [t2])
        tt(xsD[:].rearrange("p (h q) -> p h q", h=8), xs3, bcast(dsk[:, hs].unsqueeze(2), [128, 8, 64]), ALU.mult,
           [(xs_tok, sub), dsk], [xsD])
        tt(t2[:], t2[:], xsD[:], ALU.add, [t2, xsD], [t2])
        tt(t2[:], t2[:], sz[:, sub, :], ALU.mult, [t2, (sz, sub)], [t2])
        act(t1[:], t2[:], AF.Square, [t2], [t1, ssq], accum_out=ssq[:, 0:1])
        act(rs_m[:], ssq[:], AF.Ln, [ssq, epsrms], [rs_m], bias=epsrms[:, 0:1], scale=1.0 / 512.0)
        act(rs_m[:], rs_m[:], AF.Exp, [rs_m], [rs_m], scale=-0.5)
        ts(yb_tok[:], t2[:], rs_m[:, 0:1], None, ALU.mult, None, [t2, rs_m], [yb_tok])
        bt_ = nb()
        for c in range(4):
            mm(bt_[:, c * 128:(c + 1) * 128], yb_tok[:, c * 128:(c + 1) * 128], identb[:], True, True, [yb_tok, identb], [bt_])
        cp(ybT[:, g * 4:(g + 1) * 4, tc_], bt_[:].rearrange("p (a t) -> p a t", a=4), [bt_], [(ybT, (g, sub))], eng="act")
        bs = nb()
        mm(bs[:, 0:512], B_tok[:, sub, g, :], xdtw[:], True, True, [B_tok, xdtw], [bs])
        tt(Sm[:, g, :].rearrange("p (h q) -> p h q", h=8), Sm[:, g, :].rearrange("p (h q) -> p h q", h=8),
           bcast(ealast[:, sub, hs].unsqueeze(2), [128, 8, 64]), ALU.mult, [(Sm, g), (ealast, sub)], [(Sm, g)])
        tt(Sm[:, g, :], Sm[:, g, :], bs[:, 0:512], ALU.add, [(Sm, g), bs], [(Sm, g)])
        cp(Smbf[:, g, :], Sm[:, g, :], [(Sm, g)], [(Smbf, g)], eng="act")
        yield

    def hgrn_core(sub, half):
        hs = slice(half * 4, (half + 1) * 4)
        tc_ = slice(sub * 128, (sub + 1) * 128)
        tt(ff[:], sig_h[:, :, tc_], bcast(oml[:, hs].unsqueeze(2), [128, 4, 128]), ALU.mult, [sig_h, oml], [ff])
        tt(ff[:], ff[:], bcast(lb[:, hs].unsqueeze(2), [128, 4, 128]), ALU.add, [ff, lb], [ff])
        act(kk[:], ff[:], AF.Identity, [ff], [kk], bias=1.0, scale=-1.0)
        act(ff[:], ff[:], AF.Ln, [ff], [ff])
        bcf = bc[:].rearrange("p a b c -> p (a b c)")
        P.add("dve", lambda e: e.tensor_tensor_scan(out=bcf, data0=smask[:], data1=ff[:].rearrange("p a t -> p (a t)"),
                                                    initial=0.0, op0=ALU.mult, op1=ALU.add), reads=[smask, ff], writes=[bc], cost=650.0)
        tt(dd[:], bc[:], bcast(bc[:, :, :, 63:64], [128, 4, 2, 64]), ALU.subtract, [bc], [dd])
        ddf = dd[:].rearrange("p a b c -> p a (b c)")
        act(E1[:], ddf, AF.Exp, [dd], [E1])
        act(ddf, ddf, AF.Exp, [dd], [dd], scale=-1.0)
        act(elast[:].unsqueeze(3), bc[:, :, :, 63:64], AF.Exp, [bc], [elast])
        stt(qh[:], sq_h[:, :, tc_], 128.0 ** -0.5, E1[:], ALU.mult, ALU.mult, [sq_h, E1], [qh])
        tt(kh[:], kk[:], ddf, ALU.mult, [kk, dd], [kh])
        yield
        b = nb()
        for c in range(4):
            mm(b[:, c * 128:(c + 1) * 128], kh[:, c, :], identb[:], True, True, [kh, identb], [b])
        cp(kh_tok[:], b[:].rearrange("p (a t) -> p a t", a=4), [b], [kh_tok], eng="act")
        b = nb()
        for c in range(4):
            mm(b[:, c * 128:(c + 1) * 128], kh[:, c, :], qh[:, c, :], True, True, [kh, qh], [b])
        tt(scm[:], b[:].rearrange("p (a t) -> p a t", a=4), bcast(hmask[:].unsqueeze(1), [128, 4, 128]), ALU.mult, [b, hmask], [scm])
        ob = nb()
        dsb = nb()
        for hl in range(4):
            mm(ob[:, hl * 128:(hl + 1) * 128], v_h[:, sub, hl * 128:(hl + 1) * 128], scm[:, hl, :], hl == 0, False,
               [(v_h, sub), scm], [ob], sgc=True)
        for c in range(2):
            tt(Sh[:, hs, :], Sh[:, hs, :], bcast(elast[:, :, c:c + 1], [128, 4, 128]), ALU.mult, [(Sh, half), elast], [(Sh, half)])
            cp(Shbf[:, hs, :], Sh[:, hs, :], [(Sh, half)], [(Shbf, half)], eng="act")
            for hl in range(4):
                mm(ob[:, hl * 128 + c * 64: hl * 128 + (c + 1) * 64], Shbf[:, half * 4 + hl, :], qh[:, hl, c * 64:(c + 1) * 64],
                   False, c == 1, [(Shbf, half), qh], [ob], sgc=True)
            for hl in range(4):
                mm(dsb[:, hl * 128:(hl + 1) * 128], kh_tok[c * 64:(c + 1) * 64, hl, :],
                   v_h[c * 64:(c + 1) * 64, sub, hl * 128:(hl + 1) * 128], True, True, [kh_tok, (v_h, sub)], [dsb])
            tt(Sh[:, hs, :], Sh[:, hs, :], dsb[:].rearrange("p (a t) -> p a t", a=4), ALU.add, [(Sh, half), dsb], [(Sh, half)])
        yield
        cp(o_sb[:], ob[:].rearrange("p (a t) -> p a t", a=4), [ob], [o_sb], eng="act")
        act(osq[:], o_sb[:], AF.Square, [o_sb], [osq])
        b = nb()
        mm(b[:, 0:512], onesf[:], osq[:].rearrange("p a t -> p (a t)"), True, True, [onesf, osq], [b])
        act(osq[:], b[:].rearrange("p (a t) -> p a t", a=4), AF.Ln, [b, epsrms], [osq], bias=epsrms[:, 0:1], scale=1.0 / 128.0)
        act(osq[:], osq[:], AF.Exp, [osq], [osq], scale=-0.5)
        tt(o_sb[:], o_sb[:], osq[:], ALU.mult, [o_sb, osq], [o_sb])
        stt(haT[:, hs, tc_], o_sb[:], gnorm[:, 0:1], sg_h[:, :, tc_], ALU.mult, ALU.mult, [o_sb, gnorm, sg_h], [(haT, (half, sub))])
        yield

    def genA(st):
        Xs = [xbuf(st, sub) for sub in range(NSUB)]
        mv, rstd = cxA.mv, cxA.rstd
        for sub in range(NSUB):
            make_uT(Xs[sub], 8, 0, sub)
            yield
        if st == 0:
            dbg_tap("uT", uT[:], [128, 8, TT], [uT])
        s = wload("wdt", 0)
        for sub in range(NSUB):
            dt_chain(s, sub)
            yield
        conv_block(4, BT)
        for sub in range(NSUB):
            b2 = nb()
            for c in range(4):
                mm(b2[:, c * 128:(c + 1) * 128], BT[:, c, sub * 128:(sub + 1) * 128], identb[:], True, True, [BT, identb], [b2])
            cp(B_tok[:, sub, :, :], b2[:].rearrange("p (a t) -> p a t", a=4), [b2], [(B_tok, sub)], eng="act")
        yield
        conv_block(5, CT)
        yield
        for g in range(4):
            conv_block(g, xc)
            for sub in range(NSUB):
                b2 = nb()
                for c in range(4):
                    mm(b2[:, c * 128:(c + 1) * 128], xc[:, c, sub * 128:(sub + 1) * 128], identb[:], True, True, [xc, identb], [b2])
                cp(xs_tok[:, sub, :], b2[:, 0:512], [b2], [(xs_tok, sub)], eng="act")
            yield
            s = wload("wz", g)
            for sub in range(NSUB):
                b = proj_tm_sub(s, "wz", sub, 512)
                act(sz[:, sub, :], b[:, 0:512], AF.Silu, [b], [(sz, sub)])
            yield
            for sub in range(NSUB):
                for _ in mamba_core(sub, g):
                    yield
        if st == 0:
            dbg_tap("ybT", ybT[:], [128, 16, TT], [ybT])
        for half in range(2):
            bks = proj_fm("wf", half)
            for hb in range(2):
                act(sig_h[:, hb * 2:(hb + 1) * 2, :], b3(bks[hb]), AF.Tanh, [bks[hb]], [(sig_h, hb)], scale=0.5)
            yield
            for (fam, dst) in (("wq", sq_h), ("wg", sg_h)):
                bks = proj_fm(fam, half)
                for hb in range(2):
                    act(dst[:, hb * 2:(hb + 1) * 2, :], b3(bks[hb]), AF.Silu, [bks[hb]], [(dst, hb)])
                yield
            s = wload("wi", half)
            for sub in range(NSUB):
                b = proj_tm_sub(s, "wi", sub, 512)
                cp(v_h[:, sub, :], b[:, 0:512], [b], [(v_h, sub)], eng="act")
            yield
            for sub in range(NSUB):
                for _ in hgrn_core(sub, half):
                    yield
        if st == 0:
            dbg_tap("haT", haT[:], [128, 8, TT], [haT])
        for blk in range(2):
            for (fam, dst) in (("wga", sga), ("wgb", sgb)):
                bks = proj_fm(fam, blk)
                for hb in range(2):
                    act(dst[:, hb * 2:(hb + 1) * 2, :], b3(bks[hb]), AF.Tanh, [bks[hb]], [(dst, hb)], scale=0.5)
                yield
            s = wload("wba", blk)
            bks = [nb(), nb()]
            for c in range(4):
                b = bks[c // 2]
                for k in range(8):
                    mm(b[:, (c % 2) * TT:(c % 2 + 1) * TT], wv(s, "wba", k, c * 128, 128), haT[:, k, :], k == 0, k == 7, [s, haT], [b])
            for hb in range(2):
                stt(m1[:, hb * 2:(hb + 1) * 2, :], sga[:, hb * 2:(hb + 1) * 2, :], 1.0, b3(bks[hb]), ALU.add, ALU.mult, [bks[hb], sga], [(m1, hb)])
            yield
            s0 = wload("wbb", blk * 2 + 0)
            s1 = wload("wbb", blk * 2 + 1)
            bks = [nb(), nb()]
            for c in range(4):
                b = bks[c // 2]
                for kg, s_ in enumerate((s0, s1)):
                    for k in range(8):
                        mm(b[:, (c % 2) * TT:(c % 2 + 1) * TT], wv(s_, "wbb", k, c * 128, 128), ybT[:, kg * 8 + k, :],
                           kg == 0 and k == 0, kg == 1 and k == 7, [s_, ybT], [b])
            for hb in range(2):
                stt(m2[:, hb * 2:(hb + 1) * 2, :], sgb[:, hb * 2:(hb + 1) * 2, :], 1.0, b3(bks[hb]), ALU.add, ALU.mult, [bks[hb], sgb], [(m2, hb)])
            tt(mT[:, blk * 4:(blk + 1) * 4, :], m1[:], m2[:], ALU.add, [m1, m2], [(mT, blk)])
            yield
        if st == 0:
            dbg_tap("mT", mT[:], [128, 8, TT], [mT])
        for cb in range(2):
            s = wload("wo", cb)
            for sub in range(NSUB):
                X = Xs[sub]
                b = nb()
                for k in range(8):
                    mm(b[:, 0:512], mT[:, k, sub * 128:(sub + 1) * 128], wv(s, "wo", k, 0, 512), k == 0, k == 7, [s, mT], [b])
                stt(X[:, cb * 512:(cb + 1) * 512], X[:, cb * 512:(cb + 1) * 512], ALPHA, b[:, 0:512], ALU.mult, ALU.add, [X, b], [X])
            yield
        for sub in range(NSUB):
            X = Xs[sub]
            layernorm_stats(X)
            stt(cxA.nmr[:], mv[:, 0:1], -1.0, rstd[:], ALU.mult, ALU.mult, [mv, rstd], [cxA.nmr])
            act(X[:], X[:], AF.Identity, [X, rstd, cxA.nmr], [X], bias=cxA.nmr[:, 0:1], scale=rstd[:, 0:1])
            tt(X[:], X[:], lnt["ln1g"][:], ALU.mult, [X, lnt["ln1g"]], [X])
            tt(X[:], X[:], lnt["ln1b"][:], ALU.add, [X, lnt["ln1b"]], [X])
            if st == 0 and sub == 0:
                dbg_tap("x1", X[:], [128, 1024], [X])
            yield

    def genB(st):
        Xs = [xbuf(st, sub) for sub in range(NSUB)]
        uTb = cxB.uT
        mv, rstd = cxB.mv, cxB.rstd
        for sub in range(NSUB):
            make_uT(Xs[sub], 24, 16, sub)
            yield
        for blk in range(11):
            sg_ = wload("wgate", blk)
            su_ = wload("wup", blk)
            bg = nb()
            bu = nb()
            for c in range(2):
                for k in range(8):
                    mm(bg[:, c * TT:(c + 1) * TT], wv(sg_, "wgate", k, c * 128, 128), uTb[:, k, :], k == 0, k == 7, [sg_, uTb], [bg])
            for c in range(2):
                for k in range(8):
                    mm(bu[:, c * TT:(c + 1) * TT], wv(su_, "wup", k, c * 128, 128), uTb[:, k, :], k == 0, k == 7, [su_, uTb], [bu])
            act(ftmp[:], b3(bg), AF.Silu, [bg], [ftmp])
            tt(hmT[:, blk * 2:(blk + 1) * 2, :], ftmp[:], b3(bu), ALU.mult, [ftmp, bu], [(hmT, blk)])
            yield
        for cb in range(4):
            s0 = wload("wdown", cb * 2 + 0)
            s1 = wload("wdown", cb * 2 + 1)
            for sub in range(NSUB):
                X = Xs[sub]
                b = nb()
                for kg, s_ in enumerate((s0, s1)):
                    for k in range(11):
                        mm(b[:, 0:256], hmT[:, kg * 11 + k, sub * 128:(sub + 1) * 128], wv(s_, "wdown", k, 0, 256),
                           kg == 0 and k == 0, kg == 1 and k == 10, [s_, hmT], [b])
                stt(X[:, cb * 256:(cb + 1) * 256], X[:, cb * 256:(cb + 1) * 256], ALPHA, b[:, 0:256], ALU.mult, ALU.add, [X, b], [X])
            yield
        for sub in range(NSUB):
            X = Xs[sub]
            ti = st * NSUB + sub
            layernorm_stats(X)
            stt(cxB.nmr[:], mv[:, 0:1], -1.0, rstd[:], ALU.mult, ALU.mult, [mv, rstd], [cxB.nmr])
            act(X[:], X[:], AF.Identity, [X, rstd, cxB.nmr], [X], bias=cxB.nmr[:, 0:1], scale=rstd[:, 0:1])
            tt(X[:], X[:], lnt["ln2g"][:], ALU.mult, [X, lnt["ln2g"]], [X])
            tt(X[:], X[:], lnt["ln2b"][:], ALU.add, [X, lnt["ln2b"]], [X])
            dma("pool", out_d[ti * 128:(ti + 1) * 128, :], X[:], reads=[X], writes=[], dsem="out%d" % ((st % 2) * NSUB + sub))
            yield

    def run_all(gen, cx):
        n = 0
        CUR[0] = cx
        for _ in gen:
            n += 1
            CUR[0] = cx
        return n

    load_x(0)
    if NST > 1:
        load_x(1)
    nA = run_all(genA(0), cxA)
    NB_UNITS = 2 * NSUB + 15
    for st in range(NST):
        gb = genB(st)
        ga = genA(st + 1) if st + 1 < NST else None
        a_done = 0
        for i in range(NB_UNITS):
            CUR[0] = cxB
            next(gb)
            if ga is not None:
                target = ((i + 1) * nA) // NB_UNITS
                while a_done < target:
                    CUR[0] = cxA
                    next(ga)
                    a_done += 1
        for _ in gb:
            raise AssertionError("genB unit count mismatch")
        if ga is not None:
            for _ in ga:
                raise AssertionError("genA unit count mismatch")
        if st + 2 < NST:
            load_x(st + 2)

    P.emit(reorder=REORDER)
    if dbg is not None:
        print("est_ms", getattr(P, "est_ns", 0) / 1e6, flush=True)
    return nc, dbg_out


def host_consts():
    s = np.arange(128)[:, None]
    t = np.arange(128)[None, :]
    tri = (s <= t).astype(np.float32)
    hmask = ((s <= t) & ((s // 64) == (t // 64))).astype(np.float32)
    negm = np.where(s <= t, 0.0, -30000.0).astype(np.float32)
    negm4 = np.ascontiguousarray(np.tile(negm, (1, 4)))
    sm = np.ones((128, 512), np.float32)
    sm[:, ::64] = 0.0
    return {"ident": np.eye(128, dtype=np.float32), "tri": tri, "hmask": hmask, "negm": negm4, "smask": sm}


def host_shared(inp):
    f = lambda a: np.ascontiguousarray(np.asarray(a, dtype=np.float32))
    sh = {}
    w_in = f(inp["w_in"])[0]
    offs = np.cumsum([0, 1024, 1024, 1024, 1024, 2048, 3072, 32, 1024, 1024])
    names = ["wq", "wf", "wi", "wg", "wz", "wxbc", "wdt", "wga", "wgb"]
    for i, n in enumerate(names):
        sh[n] = blockify(w_in[:, offs[i]:offs[i + 1]], n)
    sh["wada"] = blockify(f(inp["w_ada"])[0], "wada")
    sh["wba"] = blockify(f(inp["w_branch_a"])[0], "wba")
    sh["wbb"] = blockify(f(inp["w_branch_b"])[0], "wbb")
    sh["wo"] = blockify(f(inp["w_o"])[0], "wo")
    sh["wgate"] = blockify(f(inp["w_ffn_gate"])[0], "wgate")
    sh["wup"] = blockify(f(inp["w_ffn_up"])[0], "wup")
    sh["wdown"] = blockify(f(inp["w_ffn_down"])[0], "wdown")
    sh["bada"] = f(inp["b_ada"]).reshape(1, 6144)
    lbr = f(inp["hgrn_lb"])
    sh["lbT"] = np.ascontiguousarray(lbr.reshape(2, 8, 128).transpose(2, 0, 1).reshape(128, 16))
    sh["gnorm"] = f(inp["hgrn_gnorm"])[0].reshape(128, 1)
    cw = f(inp["ssm_conv_w"])[0]
    sh["convw"] = np.ascontiguousarray(cw.reshape(4, 24, 128).transpose(2, 1, 0).reshape(128, 96))
    sh["convb"] = np.ascontiguousarray(f(inp["ssm_conv_b"])[0].reshape(24, 128).T)
    rep = lambda v, n: np.ascontiguousarray(np.broadcast_to(f(v).reshape(1, n), (128, n)))
    sh["dtb"] = rep(inp["ssm_dt_bias"][0], 32)
    sh["alog"] = rep(inp["ssm_a_log"][0], 32)
    sh["dsk"] = rep(inp["ssm_d"][0], 32)
    sh["snorm"] = np.ascontiguousarray(f(inp["ssm_norm"])[0].reshape(16, 128).T)
    sh["ln1g"] = rep(inp["ln1_g"][0], 1024)
    sh["ln1b"] = rep(inp["ln1_b"][0], 1024)
    sh["ln2g"] = rep(inp["ln2_g"][0], 1024)
    sh["ln2b"] = rep(inp["ln2_b"][0], 1024)
    sh.update(host_consts())
    return sh


def core_inputs(inp, sh, b):
    m = dict(sh)
    m["x"] = np.ascontiguousarray(np.asarray(inp["x"][b], dtype=np.float32))
    m["cT"] = np.ascontiguousarray(np.asarray(inp["c"][b], dtype=np.float32).reshape(8, 128).T)
    return m


_CACHE = {}


def kernel(**inputs):
    if "nc" not in _CACHE:
        _CACHE["nc"] = build(32)[0]
    nc = _CACHE["nc"]
    sh = host_shared(inputs)
    in_maps = [core_inputs(inputs, sh, b) for b in range(NCORE)]
    res = run_bass_kernel_spmd(nc, in_maps, core_ids=list(range(NCORE)))
    out = np.stack([np.asarray(r["out"]).reshape(SEQ, D) for r in res.results], axis=0)
    return out.astype(np.float32)
```
